# Optimizing a Trainium2 kernel written in Bass

```python
import math
import jax, jax.numpy as jnp
from jax import lax
import numpy as np

D_MODEL = 1024
BATCH = 16
SEQ = 2048
DEPTH = 4

GRID_W = 64
CTX_LEN = 256
Q_BLOCK = 128
ROPE_THETA = 10000.0
EPS = 1e-6

NA_HEADS = 4
NA_DH = 64
NA_KH = 8
NA_KW = 16
HY_W = 256
HY_SHORT = 3
HY_BANDS = 16
HY_EMB = 1 + 2 * HY_BANDS
HY_FFN = 64
HY_FAST_DECAY = 0.3
HY_SLOW_DECAY = 1.5
HY_TARGET = 1e-2
GQA_HEADS = 4
GQA_KV = 2
GQA_DH = 64
DIFF_HEADS = 4
DIFF_DH = 32
DIFF_DV = 64
N_BRANCH = 4
BRANCH_W = 256
PEER_HEADS = 8
PEER_NKEYS = 128
PEER_DK = 128
PEER_TOPK = 16
PEER_EXPERTS = PEER_NKEYS * PEER_NKEYS
PEER_CHUNK = 128

NA_W = NA_HEADS * NA_DH
GQA_QW = GQA_HEADS * GQA_DH
GQA_KVW = GQA_KV * GQA_DH
DIFF_QW = DIFF_HEADS * 2 * DIFF_DH
DIFF_VW = DIFF_HEADS * DIFF_DV
_SPLITS = (NA_W, NA_W, NA_W, 3 * HY_W, GQA_QW, GQA_KVW, GQA_KVW, DIFF_QW, DIFF_QW, DIFF_VW, N_BRANCH * D_MODEL)
SPLIT_IDX = tuple(int(s) for s in np.cumsum(_SPLITS)[:-1])
IN_W = int(sum(_SPLITS))

kernel_name = 'hybrid_diffusion_na_hyena_gqa_diff_peer'


def rmsnorm(x, g):
    x32 = x.astype(jnp.float32)
    y = x32 * lax.rsqrt(jnp.mean(x32 * x32, axis=-1, keepdims=True) + EPS)
    return (y * g.astype(jnp.float32)).astype(x.dtype)


def modulate(x, shift, scale):
    return x * (1 + scale) + shift


def grid_positions(n):
    t = jnp.arange(n)
    return (t // GRID_W).astype(jnp.float32), (t % GRID_W).astype(jnp.float32)


def rope_1d(x, pos):
    half = x.shape[-1] // 2
    freqs = ROPE_THETA ** (-jnp.arange(half, dtype=jnp.float32) / half)
    ang = pos[:, None] * freqs[None, :]
    cos, sin = jnp.cos(ang), jnp.sin(ang)
    x32 = x.astype(jnp.float32)
    x1, x2 = x32[..., :half], x32[..., half:]
    return jnp.concatenate([x1 * cos - x2 * sin, x1 * sin + x2 * cos], axis=-1).astype(x.dtype)


def rope_2d(x, rows, cols):
    h = x.shape[-1] // 2
    return jnp.concatenate([rope_1d(x[..., :h], rows), rope_1d(x[..., h:], cols)], axis=-1)


def to_heads(t, h):
    b, n, _ = t.shape
    return t.reshape(b, n, h, -1).transpose(0, 2, 1, 3)


def from_heads(t):
    b, h, n, d = t.shape
    return t.transpose(0, 2, 1, 3).reshape(b, n, h * d)


def group_q(q):
    b, h, n, d = q.shape
    return q.reshape(b, GQA_KV, h // GQA_KV, n, d)


def diff_heads(p):
    b, n, _ = p.shape
    return p.reshape(b, n, DIFF_HEADS, 2, DIFF_DH).transpose(0, 2, 3, 1, 4)


def attend(q, k, v):
    b, hk, g, n, dh = q.shape
    nb = n // Q_BLOCK
    k32 = k.astype(jnp.float32)
    v32 = v.astype(jnp.float32)
    scale = dh ** -0.5
    qb = jnp.moveaxis(q.reshape(b, hk, g, nb, Q_BLOCK, dh), 3, 0)

    def block(qblk):
        s = jnp.einsum('bhgqd,bhmd->bhgqm', qblk.astype(jnp.float32), k32) * scale
        p = jax.nn.softmax(s, axis=-1)
        return jnp.einsum('bhgqm,bhmd->bhgqd', p, v32)

    o = lax.map(block, qb)
    return jnp.moveaxis(o, 0, 3).reshape(b, hk, g, n, -1).astype(q.dtype)


def neighborhood_attention(q, k, v, kc, vc, rpb):
    b, h, n, dh = q.shape
    nrow = n // GRID_W
    kh = min(NA_KH, nrow)
    scale = dh ** -0.5
    f32 = jnp.float32
    qg = q.astype(f32).reshape(b, h, nrow, GRID_W, dh)
    kg = k.astype(f32).reshape(b, h, nrow, GRID_W, dh)
    vg = v.astype(f32).reshape(b, h, nrow, GRID_W, dh)
    kc32 = kc.astype(f32)
    vc32 = vc.astype(f32)
    rpb32 = rpb.astype(f32)
    nctx = kc.shape[2]
    col = jnp.arange(GRID_W)
    col_idx = jnp.clip(col - NA_KW // 2, 0, GRID_W - NA_KW)[:, None] + jnp.arange(NA_KW)[None, :]
    dc = col_idx - col[:, None] + NA_KW - 1

    def row(r):
        rs = jnp.clip(r - kh // 2, 0, nrow - kh)
        qr = lax.dynamic_index_in_dim(qg, r, axis=2, keepdims=False)
        kb = lax.dynamic_slice_in_dim(kg, rs, kh, axis=2)[:, :, :, col_idx]
        vb = lax.dynamic_slice_in_dim(vg, rs, kh, axis=2)[:, :, :, col_idx]
        dr = rs + jnp.arange(kh) - r + NA_KH - 1
        bias = rpb32[:, dr[:, None, None], dc[None, :, :]]
        s_loc = jnp.einsum('bhwd,bhiwjd->bhwij', qr, kb) * scale + jnp.transpose(bias, (0, 2, 1, 3))[None]
        s_loc = s_loc.reshape(b, h, GRID_W, kh * NA_KW)
        s_ctx = jnp.einsum('bhwd,bhcd->bhwc', qr, kc32) * scale
        p = jax.nn.softmax(jnp.concatenate([s_ctx, s_loc], axis=-1), axis=-1)
        o = jnp.einsum('bhwc,bhcd->bhwd', p[..., :nctx], vc32)
        p_loc = p[..., nctx:].reshape(b, h, GRID_W, kh, NA_KW)
        return o + jnp.einsum('bhwij,bhiwjd->bhwd', p_loc, vb)

    out = lax.map(row, jnp.arange(nrow))
    return jnp.transpose(out, (1, 2, 0, 3, 4)).reshape(b, h, n, dh).astype(q.dtype)


def hyena_filters(n, w1, b1, w2, b2, w3, b3, w4, freq):
    f = lambda a: a.astype(jnp.float32)
    t = jnp.linspace(0.0, 1.0, n, dtype=jnp.float32)[:, None]
    w = (2.0 * math.pi / n) * jnp.arange(n, dtype=jnp.float32)[:, None]
    bands = jnp.linspace(1e-4, HY_BANDS - 1, HY_BANDS, dtype=jnp.float32)[None, :]
    z = jnp.concatenate([t, jnp.cos(w * bands), jnp.sin(w * bands)], axis=-1)
    hh = jnp.sin(f(freq[0]) * (z @ f(w1) + f(b1)))
    hh = jnp.sin(f(freq[1]) * (hh @ f(w2) + f(b2)))
    hh = jnp.sin(f(freq[2]) * (hh @ f(w3) + f(b3)))
    hh = hh @ f(w4)
    deltas = jnp.linspace(math.log(HY_TARGET) / HY_SLOW_DECAY, math.log(HY_TARGET) / HY_FAST_DECAY, HY_W, dtype=jnp.float32)
    deltas = jnp.tile(jnp.abs(deltas), 2)
    hh = hh * jnp.exp(-t * deltas[None, :])
    hh = hh / jnp.sum(jnp.abs(hh), axis=0, keepdims=True)
    return hh[:, :HY_W], hh[:, HY_W:]


def fft_conv_bidir(u, hf, hb):
    n = u.shape[1]
    m = 2 * n
    u32 = u.astype(jnp.float32)
    yf = jnp.fft.irfft(jnp.fft.rfft(u32, n=m, axis=1) * jnp.fft.rfft(hf, n=m, axis=0)[None], n=m, axis=1)[:, :n]
    yb = jnp.fft.irfft(jnp.fft.rfft(u32[:, ::-1], n=m, axis=1) * jnp.fft.rfft(hb, n=m, axis=0)[None], n=m, axis=1)[:, :n][:, ::-1]
    return yf + yb


def short_conv(u, w, b):
    up = jnp.pad(u, ((0, 0), (1, 1), (0, 0)))
    return up[:, :-2] * w[0] + up[:, 1:-1] * w[1] + up[:, 2:] * w[2] + b


def hyena(u, w_short, b_short, filt, bias):
    uc = short_conv(u, w_short, b_short)
    x0, x1, v = jnp.split(uc, 3, axis=-1)
    z = x1 * v
    y = fft_conv_bidir(z, filt[0], filt[1]) + z.astype(jnp.float32) * bias.astype(jnp.float32)
    return (x0.astype(jnp.float32) * y).astype(u.dtype)


def diff_attention(q, k, v, lam, lam_init, g):
    a1 = attend(q[:, :, 0:1], k[:, :, 0], v)[:, :, 0]
    a2 = attend(q[:, :, 1:2], k[:, :, 1], v)[:, :, 0]
    o = a1.astype(jnp.float32) - lam * a2.astype(jnp.float32)
    return (rmsnorm(o, g) * (1.0 - lam_init)).astype(v.dtype)


def merge_branches(ys, gate_logits, wb, wo):
    y = jnp.stack(ys, axis=2)
    proj = jnp.einsum('bnkw,kwd->bnkd', y, wb)
    b, n, _ = gate_logits.shape
    g = jax.nn.sigmoid(gate_logits.reshape(b, n, N_BRANCH, -1))
    return jnp.sum(g * proj, axis=2) @ wo


def peer(x, wq, keys, u_tab, v_tab):
    t, d = x.shape
    f32 = jnp.float32
    q = (x @ wq).astype(f32).reshape(t, PEER_HEADS, 2, PEER_DK // 2)
    s = jnp.einsum('thpd,hpnd->thpn', q, keys.astype(f32))
    sv, si = lax.top_k(s, PEER_TOPK)
    cand = (sv[:, :, 0, :, None] + sv[:, :, 1, None, :]).reshape(t, PEER_HEADS, PEER_TOPK * PEER_TOPK)
    cidx = (si[:, :, 0, :, None] * PEER_NKEYS + si[:, :, 1, None, :]).reshape(t, PEER_HEADS, PEER_TOPK * PEER_TOPK)
    best, pos = lax.top_k(cand, PEER_TOPK)
    eidx = jnp.take_along_axis(cidx, pos, axis=-1)
    g = jax.nn.softmax(best, axis=-1)
    nc = t // PEER_CHUNK
    xs = x.reshape(nc, PEER_CHUNK, d)
    es = eidx.reshape(nc, PEER_CHUNK, PEER_HEADS * PEER_TOPK)
    gs = g.reshape(nc, PEER_CHUNK, PEER_HEADS * PEER_TOPK)

    def chunk(a):
        xc, ec, gc = a
        act = jax.nn.gelu(jnp.einsum('tkd,td->tk', u_tab[ec].astype(f32), xc.astype(f32)), approximate=False)
        return jnp.einsum('tk,tkd->td', gc * act, v_tab[ec].astype(f32))

    out = lax.map(chunk, (xs, es, gs)).reshape(t, d)
    return out.astype(x.dtype)


def setup_inputs(seed: int = 0) -> dict:
    key = jax.random.key(seed)
    ks = iter(jax.random.split(key, 48))
    D = D_MODEL

    def nrm(shape, scale):
        return jax.random.normal(next(ks), shape, jnp.float32) * scale

    return {
        'x': nrm((BATCH, SEQ, D), 1.0),
        'c': nrm((BATCH, D), 1.0),
        'ctx': nrm((BATCH, CTX_LEN, D), 1.0),
        'c_ctx': nrm((D,), 1.0),
        'w_mod': nrm((DEPTH, D, 6 * D), 0.2 * D ** -0.5),
        'b_mod': nrm((DEPTH, 6 * D), 0.02),
        'norm1_g': 1.0 + nrm((DEPTH, D), 0.02),
        'norm2_g': 1.0 + nrm((DEPTH, D), 0.02),
        'w_in': nrm((DEPTH, D, IN_W), D ** -0.5),
        'na_rpb': nrm((DEPTH, NA_HEADS, 2 * NA_KH - 1, 2 * NA_KW - 1), 0.1),
        'hy_short_w': nrm((DEPTH, HY_SHORT, 3 * HY_W), HY_SHORT ** -0.5),
        'hy_short_b': nrm((DEPTH, 3 * HY_W), 0.02),
        'hy_w1': nrm((DEPTH, HY_EMB, HY_FFN), HY_EMB ** -0.5),
        'hy_b1': nrm((DEPTH, HY_FFN), 0.02),
        'hy_w2': nrm((DEPTH, HY_FFN, HY_FFN), HY_FFN ** -0.5),
        'hy_b2': nrm((DEPTH, HY_FFN), 0.02),
        'hy_w3': nrm((DEPTH, HY_FFN, HY_FFN), HY_FFN ** -0.5),
        'hy_b3': nrm((DEPTH, HY_FFN), 0.02),
        'hy_w4': nrm((DEPTH, HY_FFN, 2 * HY_W), HY_FFN ** -0.5),
        'hy_freq': 1.0 + nrm((DEPTH, 3, HY_FFN), 0.01),
        'hy_bias': nrm((DEPTH, HY_W), 1.0),
        'gqa_qn': 1.0 + nrm((DEPTH, GQA_DH), 0.02),
        'gqa_kn': 1.0 + nrm((DEPTH, GQA_DH), 0.02),
        'diff_lq1': nrm((DEPTH, DIFF_DH), 0.1),
        'diff_lk1': nrm((DEPTH, DIFF_DH), 0.1),
        'diff_lq2': nrm((DEPTH, DIFF_DH), 0.1),
        'diff_lk2': nrm((DEPTH, DIFF_DH), 0.1),
        'diff_subln': 1.0 + nrm((DEPTH, DIFF_DV), 0.02),
        'w_branch': nrm((DEPTH, N_BRANCH, BRANCH_W, D), BRANCH_W ** -0.5),
        'w_out': nrm((DEPTH, D, D), D ** -0.5),
        'peer_wq': nrm((DEPTH, D, PEER_HEADS * PEER_DK), D ** -0.5),
        'peer_keys': nrm((DEPTH, PEER_HEADS, 2, PEER_NKEYS, PEER_DK // 2), (PEER_DK // 2) ** -0.5),
        'peer_u': nrm((DEPTH, PEER_EXPERTS, D), D ** -0.5),
        'peer_v': nrm((DEPTH, PEER_EXPERTS, D), 0.5),
        'final_g': 1.0 + nrm((D,), 0.02),
    }


def reference(x, c, ctx, c_ctx, w_mod, b_mod, norm1_g, norm2_g, w_in, na_rpb, hy_short_w, hy_short_b, hy_w1, hy_b1, hy_w2, hy_b2, hy_w3, hy_b3, hy_w4, hy_freq, hy_bias, gqa_qn, gqa_kn, diff_lq1, diff_lk1, diff_lq2, diff_lk2, diff_subln, w_branch, w_out, peer_wq, peer_keys, peer_u, peer_v, final_g):
    f32 = jnp.float32
    B, L, D = x.shape
    C = ctx.shape[1]
    rows, cols = grid_positions(L)
    h_lat, h_ctx = x, ctx
    for l in range(DEPTH):
        need_ctx = l < DEPTH - 1
        lam_init = 0.8 - 0.6 * math.exp(-0.3 * l)
        mod_lat = jnp.split((jax.nn.silu(c) @ w_mod[l] + b_mod[l])[:, None, :], 6, axis=-1)
        mod_ctx = jnp.split((jax.nn.silu(c_ctx) @ w_mod[l] + b_mod[l])[None, None, :], 6, axis=-1)

        n_lat = modulate(rmsnorm(h_lat, norm1_g[l]), mod_lat[0], mod_lat[1])
        n_ctx = modulate(rmsnorm(h_ctx, norm1_g[l]), mod_ctx[0], mod_ctx[1])
        p_lat = jnp.split(n_lat @ w_in[l], SPLIT_IDX, axis=-1)
        p_ctx = jnp.split(n_ctx @ w_in[l], SPLIT_IDX, axis=-1)

        qa, ka, va = [to_heads(t, NA_HEADS) for t in p_lat[0:3]]
        qac, kac, vac = [to_heads(t, NA_HEADS) for t in p_ctx[0:3]]
        ya_lat = from_heads(neighborhood_attention(qa, ka, va, kac, vac, na_rpb[l]))

        filt_w = (hy_w1[l], hy_b1[l], hy_w2[l], hy_b2[l], hy_w3[l], hy_b3[l], hy_w4[l], hy_freq[l])
        yb_lat = hyena(p_lat[3], hy_short_w[l], hy_short_b[l], hyena_filters(L, *filt_w), hy_bias[l])

        qg = rope_2d(rmsnorm(to_heads(p_lat[4], GQA_HEADS), gqa_qn[l]), rows, cols)
        kg = rope_2d(rmsnorm(to_heads(p_lat[5], GQA_KV), gqa_kn[l]), rows, cols)
        vg = to_heads(p_lat[6], GQA_KV)
        qgc = rmsnorm(to_heads(p_ctx[4], GQA_HEADS), gqa_qn[l])
        kgc = rmsnorm(to_heads(p_ctx[5], GQA_KV), gqa_kn[l])
        vgc = to_heads(p_ctx[6], GQA_KV)
        yc_lat = from_heads(attend(group_q(qg), jnp.concatenate([kgc, kg], axis=2), jnp.concatenate([vgc, vg], axis=2)).reshape(B, GQA_HEADS, L, GQA_DH))

        qd = rope_2d(diff_heads(p_lat[7]), rows, cols)
        kd = rope_2d(diff_heads(p_lat[8]), rows, cols)
        vd = to_heads(p_lat[9], DIFF_HEADS)
        qdc = diff_heads(p_ctx[7])
        kdc = diff_heads(p_ctx[8])
        vdc = to_heads(p_ctx[9], DIFF_HEADS)
        lam = (jnp.exp(jnp.sum(diff_lq1[l].astype(f32) * diff_lk1[l].astype(f32)))
               - jnp.exp(jnp.sum(diff_lq2[l].astype(f32) * diff_lk2[l].astype(f32))) + lam_init)
        yd_lat = from_heads(diff_attention(qd, jnp.concatenate([kdc, kd], axis=3), jnp.concatenate([vdc, vd], axis=2), lam, lam_init, diff_subln[l]))

        h_lat = h_lat + mod_lat[2] * merge_branches([ya_lat, yb_lat, yc_lat, yd_lat], p_lat[10], w_branch[l], w_out[l])

        if need_ctx:
            ya_ctx = from_heads(attend(qac[:, :, None], kac, vac)[:, :, 0])
            yb_ctx = hyena(p_ctx[3], hy_short_w[l], hy_short_b[l], hyena_filters(C, *filt_w), hy_bias[l])
            yc_ctx = from_heads(attend(group_q(qgc), kgc, vgc).reshape(B, GQA_HEADS, C, GQA_DH))
            yd_ctx = from_heads(diff_attention(qdc, kdc, vdc, lam, lam_init, diff_subln[l]))
            h_ctx = h_ctx + mod_ctx[2] * merge_branches([ya_ctx, yb_ctx, yc_ctx, yd_ctx], p_ctx[10], w_branch[l], w_out[l])

        m_lat = modulate(rmsnorm(h_lat, norm2_g[l]), mod_lat[3], mod_lat[4]).reshape(B * L, D)
        if need_ctx:
            m_ctx = modulate(rmsnorm(h_ctx, norm2_g[l]), mod_ctx[3], mod_ctx[4]).reshape(B * C, D)
            o = peer(jnp.concatenate([m_ctx, m_lat], axis=0), peer_wq[l], peer_keys[l], peer_u[l], peer_v[l])
            h_ctx = h_ctx + mod_ctx[5] * o[:B * C].reshape(B, C, D)
            o_lat = o[B * C:]
        else:
            o_lat = peer(m_lat, peer_wq[l], peer_keys[l], peer_u[l], peer_v[l])
        h_lat = h_lat + mod_lat[5] * o_lat.reshape(B, L, D)
    return rmsnorm(h_lat, final_g)
```

```python
import math
from contextlib import ExitStack

import numpy as np
import ml_dtypes

import concourse.bass as bass
import concourse.mybir as mybir
from concourse.bass_utils import run_bass_kernel_spmd

F32 = mybir.dt.float32
BF16 = mybir.dt.bfloat16
I32 = mybir.dt.int32
AF = mybir.ActivationFunctionType
ALU = mybir.AluOpType
AX = mybir.AxisListType

_ESZ = {F32: 4, BF16: 2, I32: 4, mybir.dt.uint32: 4, mybir.dt.float32r: 4,
        mybir.dt.int16: 2, mybir.dt.uint16: 2, mybir.dt.uint8: 1, mybir.dt.int8: 1}


def _prod(xs):
    r = 1
    for x in xs:
        r *= int(x)
    return r


class KB:
    NDS = 8

    def __init__(self, nc, es):
        self.nc = nc
        self.es = es
        self.e = dict(pe=nc.tensor, dve=nc.vector, act=nc.scalar, pool=nc.gpsimd, sp=nc.sync)
        self.sems = {}
        for n in ('pe', 'dve', 'act', 'pool'):
            self.sems[('c', n, 0)] = es.enter_context(nc.semaphore('c_' + n))
        self.ccnt = {n: 0 for n in ('pe', 'dve', 'act', 'pool')}
        self.cgen = {n: 0 for n in ('pe', 'dve', 'act', 'pool')}
        self.cold = {}
        self.dqs = ('sp', 'act', 'pool')
        for q in self.dqs:
            for i in range(self.NDS):
                self.sems[('d', q, i)] = es.enter_context(nc.semaphore('d_%s%d' % (q, i)))
        self.dcnt = {q: 0 for q in self.dqs}
        self.waited = {n: {} for n in self.e}
        self.acc = {}
        self.const = set()
        self.pbytes = {}
        self.n_ins = 0
        self.uid = 0

    def dram(self, name, shape, dtype, kind='Internal'):
        t = self.nc.dram_tensor(name, list(shape), dtype, kind=kind)
        if kind == 'ExternalInput':
            self.const.add(name)
        return t.ap()

    def sbuf(self, st, name, shape, dtype):
        self.uid += 1
        t = st.enter_context(self.nc.sbuf_tensor('%s_%d' % (name, self.uid), list(shape), dtype))
        return t

    def psum(self, st, name, shape, dtype):
        self.uid += 1
        t = st.enter_context(self.nc.psum_tensor('%s_%d' % (name, self.uid), list(shape), dtype))
        return t

    def _box(self, a):
        esz = _ESZ[a.dtype]
        off = int(a.offset) * esz
        aps = a.ap
        space = str(a.space)
        if space == 'DRAM':
            ext = 0
            for st, cnt in aps:
                ext += (cnt - 1) * abs(st)
            return (0, 0, off, off + ext * esz + esz)
        name = a.name
        pb = self.pbytes.get(name)
        if pb is None:
            th = a.tensor
            pb = _prod(th.shape[1:]) * _ESZ[th.dtype]
            self.pbytes[name] = pb
        p0 = off // pb
        f0 = off % pb
        pst, pcnt = aps[0]
        pstep = (pst * esz) // pb if pb else 0
        p1 = p0 + (pcnt - 1) * pstep
        ext = 0
        for st, cnt in aps[1:]:
            ext += (cnt - 1) * abs(st)
        return (p0, p1, f0, f0 + ext * esz + esz)

    @staticmethod
    def _ovl(a, b):
        return not (a[1] < b[0] or b[1] < a[0] or a[3] <= b[2] or b[3] <= a[2])

    @staticmethod
    def _contains(a, b):
        return a[0] <= b[0] and a[1] >= b[1] and a[2] <= b[2] and a[3] >= b[3]

    def _deps(self, reads, writes):
        deps = []
        for a in reads:
            if a.name in self.const:
                continue
            lst = self.acc.get(a.name)
            if not lst:
                continue
            bx = self._box(a)
            for (b, evt, w, eng) in lst:
                if w and self._ovl(bx, b):
                    deps.append(evt)
        for a in writes:
            lst = self.acc.get(a.name)
            if not lst:
                continue
            bx = self._box(a)
            for (b, evt, w, eng) in lst:
                if self._ovl(bx, b):
                    deps.append(evt)
        return deps

    def _record(self, reads, writes, evt, eng):
        for a in writes:
            bx = self._box(a)
            lst = self.acc.setdefault(a.name, [])
            lst[:] = [x for x in lst if not self._contains(bx, x[0])]
            lst.append((bx, evt, True, eng))
        for a in reads:
            if a.name in self.const:
                continue
            bx = self._box(a)
            lst = self.acc.setdefault(a.name, [])
            if evt[0][0] == 'c':
                lst[:] = [x for x in lst if not ((not x[2]) and x[3] == eng and x[1][0][0] == 'c'
                                                 and self._contains(bx, x[0]))]
            lst.append((bx, evt, False, eng))

    def _wait(self, eng, deps):
        w = self.waited[eng]
        best = {}
        for key, val in deps:
            if key[0] == 'c' and key[1] == 'pe' and eng == 'pe':
                continue
            if best.get(key, 0) < val:
                best[key] = val
        for key, val in best.items():
            if w.get(key, 0) >= val:
                continue
            self.e[eng].wait_ge(self.sems[key], val)
            w[key] = val

    def emit(self, eng, fn, reads, writes):
        self._wait(eng, self._deps(reads, writes))
        ins = fn()
        if self.ccnt[eng] >= 60000:
            self.cold[eng] = (('c', eng, self.cgen[eng]), self.ccnt[eng])
            self.cgen[eng] += 1
            self.ccnt[eng] = 0
            self.sems[('c', eng, self.cgen[eng])] = self.es.enter_context(
                self.nc.semaphore('c_%s_%d' % (eng, self.cgen[eng])))
        self.ccnt[eng] += 1
        key = ('c', eng, self.cgen[eng])
        ins.then_inc(self.sems[key], 1)
        self._record(reads, writes, (key, self.ccnt[eng]), eng)
        self.n_ins += 1
        return ins

    def dma(self, out, in_, q='sp', **kw):
        i = self.dcnt[q]
        key = ('d', q, i % self.NDS)
        deps = self._deps([in_], [out])
        if i >= self.NDS:
            deps.append((key, 16 * (i // self.NDS)))
        self._wait(q, deps)
        ins = self.e[q].dma_start(out=out, in_=in_, **kw)
        ins.then_inc(self.sems[key], 16)
        self.dcnt[q] += 1
        self._record([in_], [out], (key, 16 * (i // self.NDS + 1)), q)
        self.n_ins += 1
        return ins

    def _all_events(self):
        evts = []
        for n, c in self.ccnt.items():
            if c > 0:
                evts.append((('c', n, self.cgen[n]), c))
            elif n in self.cold:
                evts.append(self.cold[n])
        return evts

    def barrier(self):
        evts = self._all_events()
        for q in self.dqs:
            n = self.dcnt[q]
            for j in range(self.NDS):
                cnt = (n - j + self.NDS - 1) // self.NDS if n > j else 0
                if cnt > 0:
                    evts.append((('d', q, j), 16 * cnt))
        for eng in self.e:
            self._wait(eng, [e for e in evts if not (e[0][0] == 'c' and e[0][1] == eng)])
        self.acc = {}

    def finish(self):
        evts = self._all_events()
        for q in self.dqs:
            n = self.dcnt[q]
            for j in range(self.NDS):
                cnt = (n - j + self.NDS - 1) // self.NDS if n > j else 0
                if cnt > 0:
                    evts.append((('d', q, j), 16 * cnt))
        self._wait('sp', evts)

    def mm(self, out, lhsT, rhs, start=True, stop=True):
        return self.emit('pe', lambda: self.nc.tensor.matmul(out, lhsT, rhs, start=start, stop=stop),
                         [lhsT, rhs], [out])

    def tr(self, out, in_, ident):
        return self.emit('pe', lambda: self.nc.tensor.transpose(out, in_, ident), [in_, ident], [out])

    def act(self, out, in_, func, bias=None, scale=None, accum_out=None, eng='act'):
        kw = {}
        rd = [in_]
        wr = [out]
        if bias is not None:
            kw['bias'] = bias
            if not isinstance(bias, (int, float)):
                rd.append(bias)
        if scale is not None:
            kw['scale'] = scale
            if not isinstance(scale, (int, float)):
                rd.append(scale)
        if accum_out is not None:
            kw['accum_out'] = accum_out
            wr.append(accum_out)
        return self.emit('act', lambda: self.nc.scalar.activation(out, in_, func, **kw), rd, wr)

    def tt(self, out, in0, in1, op, eng='dve'):
        return self.emit(eng, lambda: self.e[eng].tensor_tensor(out, in0, in1, op), [in0, in1], [out])

    def ts(self, out, in0, s1, s2=None, op0=ALU.mult, op1=None, eng='dve', accum_out=None):
        rd = [in0]
        wr = [out]
        if not isinstance(s1, (int, float)):
            rd.append(s1)
        if s2 is not None and not isinstance(s2, (int, float)):
            rd.append(s2)
        kw = {}
        if op1 is not None:
            kw['op1'] = op1
        if accum_out is not None:
            kw['accum_out'] = accum_out
            wr.append(accum_out)
        return self.emit(eng, lambda: self.e[eng].tensor_scalar(out, in0, s1, s2, op0, **kw), rd, wr)

    def stt(self, out, in0, scalar, in1, op0, op1, accum_out=None):
        rd = [in0, in1]
        wr = [out]
        if not isinstance(scalar, (int, float)):
            rd.append(scalar)
        kw = {}
        if accum_out is not None:
            kw['accum_out'] = accum_out
            wr.append(accum_out)
        return self.emit('dve', lambda: self.nc.vector.scalar_tensor_tensor(out, in0, scalar, in1, op0, op1, **kw),
                         rd, wr)

    def copy(self, out, in_, eng='dve'):
        if eng == 'act':
            return self.emit('act', lambda: self.nc.scalar.copy(out, in_), [in_], [out])
        return self.emit(eng, lambda: self.e[eng].tensor_copy(out, in_), [in_], [out])

    def memset(self, out, val, eng='dve'):
        return self.emit(eng, lambda: self.e[eng].memset(out, val), [], [out])

    def reduce(self, out, in_, op=ALU.add, axis=AX.X, absval=None, eng='dve'):
        kw = {}
        if absval:
            kw['apply_absolute_value'] = True
        return self.emit(eng, lambda: self.e[eng].tensor_reduce(out, in_, axis, op, **kw), [in_], [out])

    def recip(self, out, in_):
        return self.emit('dve', lambda: self.nc.vector.reciprocal(out, in_), [in_], [out])

    def max8(self, out, in_):
        return self.emit('dve', lambda: self.nc.vector.max(out, in_), [in_], [out])

    def match_replace(self, out, to_replace, values, imm):
        return self.emit('dve', lambda: self.nc.vector.match_replace(out, to_replace, values, imm),
                         [to_replace, values], [out])


DEPTH = 4
D = 1024
NB = 2
CTX = 256
LAT = 2048
TPB = CTX + LAT
T = NB * TPB
NT = T // 128
GRID_W = 64
EPS = 1e-6
NEG = -30000.0
R_NAQ, R_NAK, R_HY, R_GQ, R_GQR, R_GK, R_GKR, R_DQ, R_DQR, R_DK, R_DKR, R_GATE = (
    0, 256, 512, 1280, 1536, 1792, 1920, 2048, 2304, 2560, 2816, 3072)
NFM = 7168
NTM = 640
INW2 = NFM + NTM
NF = 2176
NFC = 384


def _perm(n_total, group):
    half = group // 2
    idx = np.arange(n_total)
    return np.where((idx % group) < half, idx + half, idx - half)


def host_consts():
    c = {}
    c['ident'] = np.eye(128, dtype=np.float32)
    t = np.arange(LAT)
    rows = (t // GRID_W).astype(np.float64)
    cols = (t % GRID_W).astype(np.float64)

    def rope_tab(dh):
        h = dh // 2
        half = h // 2
        f = np.arange(dh)
        pos = np.where(f[:, None] < h, rows[None, :], cols[None, :])
        idx = (f % h) % half
        freq = 10000.0 ** (-idx.astype(np.float64) / half)
        ang = pos * freq[:, None]
        cs = np.cos(ang)
        sn = np.sin(ang)
        first = ((f % h) < half)
        sn = np.where(first[:, None], -sn, sn)
        cs_full = np.concatenate([np.ones((dh, CTX)), cs], axis=1)
        sn_full = np.concatenate([np.zeros((dh, CTX)), sn], axis=1)
        return cs_full, sn_full

    c64, s64 = rope_tab(64)
    c32, s32 = rope_tab(32)
    c32 = np.concatenate([c32, c32], axis=0)
    s32 = np.concatenate([s32, s32], axis=0)
    sc = 32.0 ** -0.5
    c['rope'] = np.stack([c64, s64, c32 * sc, s32 * sc, c32, s32]).astype(np.float32)
    for nm, n in (('lat', LAT), ('ctx', CTX)):
        tt_ = np.linspace(0.0, 1.0, n, dtype=np.float32)[:, None]
        w = (np.float32(2.0 * math.pi / n) * np.arange(n, dtype=np.float32))[:, None]
        bands = np.linspace(1e-4, 16 - 1, 16, dtype=np.float32)[None, :]
        z = np.concatenate([tt_, np.cos(w * bands), np.sin(w * bands)], axis=-1).astype(np.float32)
        c['hz_' + nm] = np.ascontiguousarray(z.T)
        deltas = np.linspace(math.log(1e-2) / 1.5, math.log(1e-2) / 0.3, 256, dtype=np.float32)
        deltas = np.tile(np.abs(deltas), 2)
        win = np.exp(-tt_ * deltas[None, :]).astype(np.float32)
        c['hwin_' + nm] = np.ascontiguousarray(win.T.reshape(4, 128, n).transpose(1, 0, 2))
    for nm, n, nf in (('lat', LAT, NF), ('ctx', CTX, NFC)):
        N = 2 * n
        a = np.arange(nf)[:, None].astype(np.float64)
        b = np.arange(n)[None, :].astype(np.float64)
        ang = 2.0 * math.pi * ((a * b) % N) / N
        Cr = np.cos(ang)
        Sr = np.sin(ang)
        c['crow_' + nm] = Cr.astype(ml_dtypes.bfloat16)
        c['srow_' + nm] = Sr.astype(ml_dtypes.bfloat16)
        nsc = n // 128
        nfc = nf // 128
        Cf = Cr.T.reshape(nsc, 128, nfc, 128).transpose(2, 1, 0, 3)
        Sf = Sr.T.reshape(nsc, 128, nfc, 128).transpose(2, 1, 0, 3)
        c['cfwd_' + nm] = np.ascontiguousarray(Cf).astype(ml_dtypes.bfloat16)
        c['sfwd_' + nm] = np.ascontiguousarray(Sf).astype(ml_dtypes.bfloat16)
        wg = np.full((nf,), 2.0 / N)
        wg[0] = 1.0 / N
        wg[n] = 1.0 / N
        wg[n + 1:] = 0.0
        c['wgt_' + nm] = np.ascontiguousarray(wg.reshape(nfc, 128).T).astype(np.float32)
    return c


def host_layout(inp):
    L = {}
    w_in = inp['w_in']
    p64q = _perm(256, 32)
    p64k = _perm(128, 32)
    p32 = _perm(256, 16)
    cols = np.concatenate([
        np.arange(0, 256), np.arange(256, 512), np.arange(768, 1536),
        1536 + np.arange(256), 1536 + p64q, 1792 + np.arange(128), 1792 + p64k,
        2048 + np.arange(256), 2048 + p32, 2304 + np.arange(256), 2304 + p32,
        np.arange(2816, 6912),
        np.arange(512, 768), np.arange(1920, 2048), np.arange(2560, 2816)])
    assert cols.shape[0] == INW2
    L['w_in'] = np.ascontiguousarray(w_in[:, :, cols])
    L['w_mod'] = inp['w_mod']
    L['b_mod'] = inp['b_mod']
    L['g12'] = np.ascontiguousarray(np.stack([inp['norm1_g'], inp['norm2_g']], axis=1))
    L['final_g'] = inp['final_g'].reshape(1, D)
    rpb = inp['na_rpb']
    ck = np.arange(64)[:, None]
    cq = np.arange(64)[None, :]
    cs = np.clip(cq - 8, 0, 48)
    valid = (ck >= cs) & (ck < cs + 16)
    dc = np.clip(ck - cq + 15, 0, 30)
    tc = np.where(valid[None, None, None], rpb[:, :, :, dc], np.float32(NEG))
    top = tc[:, :, 0:14].transpose(0, 1, 3, 2, 4)
    bot = tc[:, :, 1:15].transpose(0, 1, 3, 2, 4)
    L['na_tc'] = np.ascontiguousarray(np.concatenate([top, bot], axis=2).transpose(0, 2, 1, 3, 4)).astype(np.float32)
    sw = inp['hy_short_w']
    sb = inp['hy_short_b']
    swb = np.concatenate([sw, sb[:, None, :]], axis=1)
    L['hy_sw'] = np.ascontiguousarray(swb.reshape(DEPTH, 4, 6, 128).transpose(0, 3, 2, 1))
    L['hy_w1'] = inp['hy_w1']
    L['hy_w2'] = inp['hy_w2']
    L['hy_w3'] = inp['hy_w3']
    L['hy_w4'] = inp['hy_w4']
    hb = np.stack([inp['hy_b1'], inp['hy_b2'], inp['hy_b3']], axis=1)
    L['hy_bf'] = np.ascontiguousarray(np.concatenate([hb, inp['hy_freq']], axis=1).transpose(0, 2, 1))
    L['hy_bias'] = np.ascontiguousarray(inp['hy_bias'].reshape(DEPTH, 2, 128).transpose(0, 2, 1))
    pq = _perm(64, 32)
    L['gqa_g'] = np.ascontiguousarray(np.stack([inp['gqa_qn'], inp['gqa_qn'][:, pq], inp['gqa_kn'], inp['gqa_kn'][:, pq]], axis=2))
    L['diff_l'] = np.ascontiguousarray(np.stack([inp['diff_lq1'], inp['diff_lk1'], inp['diff_lq2'], inp['diff_lk2']], axis=1))
    L['diff_subln'] = inp['diff_subln']
    L['w_branch'] = inp['w_branch']
    L['w_out'] = inp['w_out']
    L['peer_wq'] = inp['peer_wq']
    keys = inp['peer_keys']
    kb = np.zeros((DEPTH, 128, 8, 256), np.float32)
    for p in range(2):
        kb[:, p * 64:(p + 1) * 64, :, p * 128:(p + 1) * 128] = keys[:, :, p].transpose(0, 3, 1, 2)
    L['peer_kb'] = kb
    L['peer_u'] = inp['peer_u']
    L['peer_v'] = inp['peer_v']
    return L


class _Off:
    def __init__(self, t, off):
        self.t, self.off = t, off

    def __getitem__(self, key):
        key = list(key)
        key[1] = key[1] - self.off
        return self.t[tuple(key)]


def dram_bc(ap1d, p):
    n = ap1d.size()
    return bass.AP(tensor=ap1d.tensor, offset=ap1d.offset, ap=[[0, p], [1, n]])


def sb_view(tile_ap, col_off, dims):
    base = tile_ap
    return bass.AP(tensor=base.tensor, offset=base.offset + col_off, ap=[list(base.ap[0])] + [list(d) for d in dims])


def row_type(i):
    b, j = divmod(i, 18)
    return 2 if j < 2 else b


class Prog:
    def __init__(self, n_layers=DEPTH, debug=False, stages=None):
        self.n_layers = n_layers
        self.debug = debug
        self.stages = stages
        self.nc = bass.Bass("TRN2", target_bir_lowering=False)
        self.es = ExitStack()
        self.kb = KB(self.nc, self.es)
        self.inputs = {}

    def ext(self, name, arr_shape, dt=F32):
        ap = self.kb.dram(name, list(arr_shape), dt, 'ExternalInput')
        self.inputs[name] = (tuple(arr_shape), dt)
        return ap

    def scratch(self, name, shape, dt):
        return self.kb.dram(name, list(shape), dt, 'ExternalOutput' if self.debug else 'Internal')

    def want(self, s):
        return self.stages is None or s in self.stages

    def build(self, shapes):
        kb = self.kb
        es = self.es
        I = {}
        for k, (shp, dt) in shapes.items():
            I[k] = self.ext(k, shp, dt)
        self.I = I
        self.out = kb.dram('out', [NB, LAT, D], F32, 'ExternalOutput')
        self.h = self.scratch('h', [T, D], F32)
        self.modx = self.scratch('modx', [3, 6144], F32)
        self.PT = self.scratch('PT', [NFM, T], F32)
        self.PV = self.scratch('PV', [T, NTM], F32)
        self.YT = self.scratch('YT', [1024, T], BF16)
        self.Hl = self.scratch('Hl', [128, NF // 128, 2, 256], F32)
        self.Hc = self.scratch('Hc', [128, NFC // 128, 2, 256], F32)
        self.UT = kb.dram('UT', [128, 8, 16384], BF16)
        self.VB = kb.dram('VB', [16384, 1024], BF16)
        self.PS = [kb.psum(es, 'ps%d' % i, [128, 512], F32) for i in range(8)]
        self.ident_f = kb.sbuf(es, 'identf', [128, 128], F32)
        self.ident_b = kb.sbuf(es, 'identb', [128, 128], BF16)
        self.ones_f = kb.sbuf(es, 'onesf', [128, 128], F32)
        kb.dma(self.ident_f[:], I['ident'])
        kb.copy(self.ident_b[:], self.ident_f[:])
        kb.memset(self.ones_f[:], 1.0)
        for i in range(4):
            r0 = i * (T // 4)
            kb.dma(self.h[r0:r0 + T // 4, :], I['h0'][r0:r0 + T // 4, :], q=('sp', 'pool')[i % 2])
        kb.barrier()
        for l in range(self.n_layers):
            need_ctx = l < DEPTH - 1
            if self.want('mod'):
                self.st_mod(l)
            if self.want('inproj'):
                self.st_inproj(l)
            if self.want('filt'):
                self.st_filters(l, need_ctx)
            if self.want('attn'):
                self.st_attn(l, need_ctx)
            if self.want('hyena'):
                self.st_hyena(l, need_ctx)
            if self.want('merge'):
                self.st_merge(l)
            if self.want('peer'):
                self.st_peer(l, need_ctx)
        if self.want('final'):
            self.st_final()
        kb.finish()
        return self.nc

    def psb(self, i):
        return self.PS[i][:].bitcast(BF16)

    def st_mod(self, l):
        kb, I = self.kb, self.I
        with ExitStack() as st:
            ct = kb.sbuf(st, 'ct', [128, 8, 3], F32)
            kb.dma(ct[:], I['cT'])
            sc = kb.sbuf(st, 'sc', [128, 8, 3], F32)
            kb.act(sc[:], ct[:], AF.Silu)
            mv = kb.sbuf(st, 'mv', [3, 6144], F32)
            bm = kb.sbuf(st, 'bm', [3, 6144], F32)
            kb.dma(bm[:], dram_bc(I['b_mod'][l], 3))
            gt = kb.sbuf(st, 'gt', [3, 2048], F32)
            kb.dma(gt[:], dram_bc(I['g12'][l].rearrange('a d -> (a d)'), 3))
            wts = [kb.sbuf(st, 'wm%d' % i, [128, 8, 512], F32) for i in range(2)]
            wv = I['w_mod'][l].rearrange('(kc p) n -> p kc n', p=128)
            for n in range(12):
                wt = wts[n % 2]
                kb.dma(wt[:], wv[:, :, n * 512:(n + 1) * 512], q=('sp', 'pool')[n % 2])
                ps = self.PS[n % 2]
                for kc in range(8):
                    kb.mm(ps[0:3, :], sc[:, kc, :], wt[:, kc, :], start=(kc == 0), stop=(kc == 7))
                kb.tt(mv[:, n * 512:(n + 1) * 512], ps[0:3, :], bm[:, n * 512:(n + 1) * 512], ALU.add)
            kb.stt(mv[:, 1024:2048], mv[:, 1024:2048], 1.0, gt[:, 0:1024], ALU.add, ALU.mult)
            kb.stt(mv[:, 4096:5120], mv[:, 4096:5120], 1.0, gt[:, 1024:2048], ALU.add, ALU.mult)
            kb.dma(self.modx, mv[:])
            kb.barrier()

    def load_bc(self, st, name, col0, n=1024):
        kb = self.kb
        t = kb.sbuf(st, name, [128, 3, n], F32)
        for r in range(3):
            kb.dma(t[:, r, :], dram_bc(self.modx[r, col0:col0 + n], 128), q=('sp', 'pool', 'sp')[r])
        return t

    def norm_tile(self, x, x2, junk, ss, src, A, B, out_bf):
        kb = self.kb
        kb.dma(x[:], src)
        kb.act(junk[:], x[:], AF.Square, accum_out=ss[:])
        kb.ts(ss[:], ss[:], 1.0 / D, EPS, op0=ALU.mult, op1=ALU.add)
        kb.act(ss[:], ss[:], AF.Sqrt)
        kb.recip(ss[:], ss[:])
        kb.stt(x2[:], x[:], ss[:], A, ALU.mult, ALU.mult)
        kb.tt(out_bf, x2[:], B, ALU.add)

    def st_inproj(self, l):
        kb, I = self.kb, self.I
        with ExitStack() as st:
            nT = kb.sbuf(st, 'nT', [128, 8, T], BF16)
            with ExitStack() as s2:
                A = self.load_bc(s2, 'A1', 1024)
                B = self.load_bc(s2, 'B1', 0)
                xs = [kb.sbuf(s2, 'x%d' % i, [128, D], F32) for i in range(2)]
                x2s = [kb.sbuf(s2, 'x2%d' % i, [128, D], F32) for i in range(2)]
                nbs = [kb.sbuf(s2, 'nb%d' % i, [128, D], BF16) for i in range(2)]
                junk = kb.sbuf(s2, 'junk', [128, D], BF16)
                sss = [kb.sbuf(s2, 'ss%d' % i, [128, 1], F32) for i in range(2)]
                for i in range(NT):
                    r = row_type(i)
                    nb = nbs[i % 2]
                    self.norm_tile(xs[i % 2], x2s[i % 2], junk, sss[i % 2], self.h[i * 128:(i + 1) * 128, :],
                                   A[:, r, :], B[:, r, :], nb[:])
                    pT = self.psb(i % 2)
                    for kc in range(8):
                        kb.tr(pT[:, kc * 128:(kc + 1) * 128], nb[:, kc * 128:(kc + 1) * 128], self.ident_b[:])
                    kb.copy(nT[:, :, i * 128:(i + 1) * 128], pT.rearrange('p (k t) -> p k t', k=8),
                            eng=('act', 'dve')[i % 2])
                kb.barrier()
            wv = I['w_in'][l].rearrange('(kc p) n -> p kc n', p=128)
            w32 = [kb.sbuf(st, 'w32%d' % i, [128, 8, 512], F32) for i in range(2)]
            wb = [kb.sbuf(st, 'wb%d' % i, [128, 8, 512], BF16) for i in range(2)]
            o32 = [kb.sbuf(st, 'o32%d' % i, [128, 512], F32) for i in range(4)]
            cnt = 0
            for cg in range(NFM // 512):
                w3 = w32[cg % 2]
                w2 = wb[cg % 2]
                kb.dma(w3[:, 0:4, :], wv[:, 0:4, cg * 512:(cg + 1) * 512], q='sp')
                kb.dma(w3[:, 4:8, :], wv[:, 4:8, cg * 512:(cg + 1) * 512], q='pool')
                kb.copy(w2[:, 0:4, :], w3[:, 0:4, :], eng='dve')
                kb.copy(w2[:, 4:8, :], w3[:, 4:8, :], eng='act')
                for blk in range(T // 512):
                    for fc in range(4):
                        ps = self.PS[2 + cnt % 4]
                        for kc in range(8):
                            kb.mm(ps[:, :], w2[:, kc, fc * 128:(fc + 1) * 128], nT[:, kc, blk * 512:(blk + 1) * 512],
                                  start=(kc == 0), stop=(kc == 7))
                        o = o32[cnt % 4]
                        kb.copy(o[:], ps[:, :], eng=('act', 'dve')[cnt % 2])
                        row = (cg * 4 + fc) * 128
                        kb.dma(self.PT[row:row + 128, blk * 512:(blk + 1) * 512], o[:], q=('sp', 'pool')[cnt % 2])
                        cnt += 1
            kb.dma(w32[0][:], wv[:, :, NFM:NFM + 512], q='sp')
            kb.dma(w32[1][:, :, 0:128], wv[:, :, NFM + 512:NFM + 640], q='pool')
            kb.copy(wb[0][:], w32[0][:], eng='dve')
            kb.copy(wb[1][:, :, 0:128], w32[1][:, :, 0:128], eng='act')
            ov = [kb.sbuf(st, 'ov%d' % i, [128, NTM], F32) for i in range(2)]
            for i in range(NT):
                pa = self.PS[2 + (2 * i) % 4]
                pb = self.PS[3 + (2 * i) % 4]
                for kc in range(8):
                    kb.mm(pa[:, :], nT[:, kc, i * 128:(i + 1) * 128], wb[0][:, kc, :], start=(kc == 0), stop=(kc == 7))
                for kc in range(8):
                    kb.mm(pb[:, 0:128], nT[:, kc, i * 128:(i + 1) * 128], wb[1][:, kc, 0:128], start=(kc == 0), stop=(kc == 7))
                o = ov[i % 2]
                kb.copy(o[:, 0:512], pa[:, :], eng='act')
                kb.copy(o[:, 512:640], pb[:, 0:128], eng='dve')
                kb.dma(self.PV[i * 128:(i + 1) * 128, :], o[:], q=('sp', 'pool')[i % 2])
            kb.barrier()

    def attn_block(self, QT, q0, nq, chunks, po, ebufs, pbase):
        kb = self.kb
        n = len(chunks)
        for i, (kt, vx, bias) in enumerate(chunks):
            ps = self.PS[pbase + self._sc % 3]
            e = ebufs[self._sc % 3]
            self._sc += 1
            kb.mm(ps[:, 0:nq], kt, QT[:, q0:q0 + nq], start=True, stop=(bias is None))
            if bias is not None:
                kb.mm(ps[:, 0:nq], self.ident_b[:], bias, start=False, stop=True)
            kb.act(e[:, 0:nq], ps[:, 0:nq], AF.Exp)
            self._q.append((lambda vx=vx, e=e, i=i: kb.mm(po, vx, e[:, 0:nq], start=(i == 0), stop=(i == n - 1))))
            self.attn_pump(2)

    def attn_pump(self, keep):
        while len(self._q) > keep:
            self._q.pop(0)()

    def attn_fin(self, po, nq, oT, dests, rc):
        def fin():
            self._attn_fin_now(po, nq, self._oTs[self._fn % 2], dests, self._rcs[self._fn % 2])
            self._fn += 1
        self._q.append(fin)

    def attn_flush(self):
        self.attn_pump(0)

    def _attn_fin_now(self, po, nq, oT, dests, rc):
        kb = self.kb
        kb.copy(oT[0:65, 0:nq], po, eng='act')
        for sbi, dst in enumerate(dests):
            pt = self.PS[5 + self._fc % 2]
            self._fc += 1
            kb.tr(pt[:, 0:65], oT[0:65, sbi * 128:(sbi + 1) * 128], self.ident_f[0:65, 0:65])
            kb.recip(rc[:], pt[:, 64:65])
            kb.ts(dst, pt[:, 0:64], rc[:], None, op0=ALU.mult)

    def tm_to_fm(self, Y, ntile, row0, col0, stg):
        kb = self.kb
        for cc in range(2):
            for g0 in range(0, ntile, 4):
                g = min(4, ntile - g0)
                pt = self.PS[5 + self._fc % 2]
                self._fc += 1
                for j in range(g):
                    kb.tr(pt[:, j * 128:(j + 1) * 128], Y[:, g0 + j, cc * 128:(cc + 1) * 128], self.ident_f[:])
                kb.copy(stg[:, cc, g0 * 128:(g0 + g) * 128], pt[:, 0:g * 128], eng=('act', 'dve')[(g0 // 4) % 2])
            kb.dma(self.YT[row0 + cc * 128:row0 + (cc + 1) * 128, col0:col0 + ntile * 128], stg[:, cc, 0:ntile * 128],
                   q=('sp', 'pool')[cc])

    def prep_rope(self, dst, xrows, xrrows, Ct, St, rms, tmp):
        kb = self.kb
        x, xr = tmp['x'], tmp['xr']
        kb.dma(x[:], xrows, q='sp')
        kb.dma(xr[:], xrrows, q='pool')
        if rms:
            sq, rs = tmp['sq'], tmp['rs']
            kb.act(sq[:], x[:], AF.Square)
            for c0 in range(0, TPB, 512):
                n = min(512, TPB - c0)
                ps = self.PS[7]
                kb.mm(ps[0:64, 0:n], self.ones_f[0:64, 0:64], sq[:, c0:c0 + n])
                kb.ts(rs[:, c0:c0 + n], ps[0:64, 0:n], 1.0 / 64, EPS, op0=ALU.mult, op1=ALU.add)
            kb.act(rs[:], rs[:], AF.Sqrt)
            kb.recip(rs[:], rs[:])
        kb.tt(x[:], x[:], Ct, ALU.mult)
        kb.tt(xr[:], xr[:], St, ALU.mult, eng='pool')
        if rms:
            kb.tt(x[:], x[:], xr[:], ALU.add)
            kb.tt(dst, x[:], rs[:], ALU.mult)
        else:
            kb.tt(dst, x[:], xr[:], ALU.add)

    def load_vx(self, VX, vst, row0, nchunk, col0, nh):
        kb = self.kb
        kb.dma(vst[:, 0:nchunk, 0:nh * 64],
               self.PV[row0:row0 + nchunk * 128, col0:col0 + nh * 64].rearrange('(kc p) c -> p kc c', p=128))
        for hh in range(nh):
            kb.copy(VX[:, 0:nchunk, hh, 0:64], vst[:, 0:nchunk, hh * 64:(hh + 1) * 64], eng=('dve', 'pool')[hh % 2])
        kb.memset(VX[:, :, :, 64:65], 1.0)

    def st_attn(self, l, need_ctx):
        kb, I = self.kb, self.I
        self._sc = 0
        self._fc = 0
        lam_init = 0.8 - 0.6 * math.exp(-0.3 * l)
        with ExitStack() as st:
            ebufs = [kb.sbuf(st, 'e%d' % i, [128, 512], BF16) for i in range(3)]
            oT = kb.sbuf(st, 'oT', [65, 512], F32)
            rc = kb.sbuf(st, 'rc', [128, 1], F32)
            self._oTs = [oT, kb.sbuf(st, 'oT2', [65, 512], F32)]
            self._rcs = [rc, kb.sbuf(st, 'rc2', [128, 1], F32)]
            self._q = []
            self._fn = 0
            Y = kb.sbuf(st, 'Y', [128, 18, 256], F32)
            stg = kb.sbuf(st, 'stg', [128, 2, TPB], BF16)
            vst = kb.sbuf(st, 'vst', [128, 18, 256], F32)
            VX = kb.sbuf(st, 'VX', [128, 18, 4, 65], BF16)
            x32 = kb.sbuf(st, 'x32', [64, TPB], F32)
            tmp = dict(x=x32, xr=kb.sbuf(st, 'xr32', [64, TPB], F32), sq=kb.sbuf(st, 'sq32', [64, TPB], F32),
                       rs=kb.sbuf(st, 'rs32', [64, TPB], F32))
            KT = kb.sbuf(st, 'KT', [64, 4, TPB], BF16)
            QT = kb.sbuf(st, 'QT', [64, TPB], BF16)
            with ExitStack() as s2:
                VXb = kb.sbuf(s2, 'VXb', [128, 15, 4, 65], BF16)
                tc32 = kb.sbuf(s2, 'tc32', [128, 4, 14, 64], F32)
                TC = kb.sbuf(s2, 'TC', [128, 4, 14, 64], BF16)
                kb.dma(tc32[:], I['na_tc'][l])
                kb.copy(TC[:], tc32[:])
                for b in range(NB):
                    c0 = b * TPB
                    for hh in range(4):
                        kb.dma(x32[:], self.PT[R_NAK + hh * 64:R_NAK + (hh + 1) * 64, c0:c0 + TPB], q=('sp', 'pool')[hh % 2])
                        kb.copy(KT[:, hh, :], x32[:], eng=('dve', 'act')[hh % 2])
                    self.load_vx(VX, vst, c0, 18, 0, 4)
                    self.load_vx(VXb, vst, c0 + CTX + 64, 15, 0, 4)
                    for hh in range(4):
                        kb.dma(x32[:], self.PT[R_NAQ + hh * 64:R_NAQ + (hh + 1) * 64, c0:c0 + TPB], q=('sp', 'pool')[hh % 2])
                        kb.ts(QT[:], x32[:], 0.125, None, op0=ALU.mult)
                        for m in range(16):
                            po = self.PS[3 + m % 2]
                            for r in (2 * m, 2 * m + 1):
                                rs_ = min(max(r - 4, 0), 24)
                                chunks = [(KT[:, hh, kc * 128:(kc + 1) * 128], VX[:, kc, hh, :], None) for kc in range(2)]
                                for i in range(4):
                                    kr = rs_ + 2 * i
                                    kt = KT[:, hh, CTX + kr * 64:CTX + kr * 64 + 128]
                                    vx = VX[:, 2 + kr // 2, hh, :] if kr % 2 == 0 else VXb[:, (kr - 1) // 2, hh, :]
                                    chunks.append((kt, vx, TC[:, hh, kr - r + 7, :]))
                                self.attn_block(QT, CTX + r * 64, 64, chunks, po[0:65, (r % 2) * 64:(r % 2) * 64 + 64], ebufs, 0)
                            self.attn_fin(po[0:65, 0:128], 128, oT, [Y[:, 2 + m, hh * 64:(hh + 1) * 64]], rc)
                        if need_ctx:
                            po = self.PS[3]
                            chunks = [(KT[:, hh, kc * 128:(kc + 1) * 128], VX[:, kc, hh, :], None) for kc in range(2)]
                            self.attn_block(QT, 0, 256, chunks, po[0:65, 0:256], ebufs, 0)
                            self.attn_fin(po[0:65, 0:256], 256, oT, [Y[:, j, hh * 64:(hh + 1) * 64] for j in range(2)], rc)
                    self.attn_flush()
                    if need_ctx:
                        self.tm_to_fm(Y, 18, 0, c0, stg)
                    else:
                        self.tm_to_fm(sb_view(Y[:], 2 * 256, [[256, 16], [1, 256]]), 16, 0, c0 + CTX, stg)
                kb.barrier()
            with ExitStack() as s2:
                gg = kb.sbuf(s2, 'gg', [64, 4], F32)
                kb.dma(gg[:], I['gqa_g'][l])
                tabs = kb.sbuf(s2, 'gtabs', [64, 4, TPB], F32)
                for j in range(4):
                    kb.dma(tabs[:, j, :], I['rope'][j % 2], q=('sp', 'pool')[j % 2])
                kb.ts(tabs[:, 0, :], tabs[:, 0, :], gg[:, 0:1], 0.125, op0=ALU.mult, op1=ALU.mult)
                kb.ts(tabs[:, 1, :], tabs[:, 1, :], gg[:, 1:2], 0.125, op0=ALU.mult, op1=ALU.mult)
                kb.ts(tabs[:, 2, :], tabs[:, 2, :], gg[:, 2:3], None, op0=ALU.mult)
                kb.ts(tabs[:, 3, :], tabs[:, 3, :], gg[:, 3:4], None, op0=ALU.mult)
                for b in range(NB):
                    c0 = b * TPB
                    for kv in range(2):
                        self.prep_rope(KT[:, kv, :], self.PT[R_GK + kv * 64:R_GK + (kv + 1) * 64, c0:c0 + TPB],
                                       self.PT[R_GKR + kv * 64:R_GKR + (kv + 1) * 64, c0:c0 + TPB],
                                       tabs[:, 2, :], tabs[:, 3, :], True, tmp)
                    self.load_vx(VX, vst, c0, 18, 256, 2)
                    for hh in range(4):
                        kv = hh // 2
                        self.prep_rope(QT[:], self.PT[R_GQ + hh * 64:R_GQ + (hh + 1) * 64, c0:c0 + TPB],
                                       self.PT[R_GQR + hh * 64:R_GQR + (hh + 1) * 64, c0:c0 + TPB],
                                       tabs[:, 0, :], tabs[:, 1, :], True, tmp)
                        blocks = [(CTX + i * 512, 512, 18) for i in range(4)]
                        if need_ctx:
                            blocks.append((0, 256, 2))
                        for bi, (q0, nq, nk) in enumerate(blocks):
                            po = self.PS[3 + bi % 2]
                            chunks = [(KT[:, kv, kc * 128:(kc + 1) * 128], VX[:, kc, kv, :], None) for kc in range(nk)]
                            self.attn_block(QT, q0, nq, chunks, po[0:65, 0:nq], ebufs, 0)
                            self.attn_fin(po[0:65, 0:nq], nq, oT,
                                          [Y[:, q0 // 128 + j, hh * 64:(hh + 1) * 64] for j in range(nq // 128)], rc)
                    self.attn_flush()
                    if need_ctx:
                        self.tm_to_fm(Y, 18, 512, c0, stg)
                    else:
                        self.tm_to_fm(sb_view(Y[:], 2 * 256, [[256, 16], [1, 256]]), 16, 512, c0 + CTX, stg)
                kb.barrier()
            with ExitStack() as s2:
                rope = kb.sbuf(s2, 'rope', [64, 6, TPB], F32) if False else _Off(kb.sbuf(s2, 'rope', [64, 4, TPB], F32), 2)
                for j in range(2, 6):
                    kb.dma(rope[:, j, :], I['rope'][j], q=('sp', 'pool')[j % 2])
                dl = kb.sbuf(s2, 'dl', [128, 4, 32], F32)
                kb.dma(dl[:], dram_bc(I['diff_l'][l].rearrange('a d -> (a d)'), 128))
                lw = kb.sbuf(s2, 'lw', [128, 2, 32], F32)
                kb.tt(lw[:, 0, :], dl[:, 0, :], dl[:, 1, :], ALU.mult)
                kb.tt(lw[:, 1, :], dl[:, 2, :], dl[:, 3, :], ALU.mult)
                l2 = kb.sbuf(s2, 'l2', [128, 2], F32)
                kb.reduce(l2[:], lw[:], ALU.add, AX.X)
                kb.act(l2[:], l2[:], AF.Exp)
                nlam = kb.sbuf(s2, 'nlam', [128, 1], F32)
                kb.tt(nlam[:], l2[:, 1:2], l2[:, 0:1], ALU.subtract)
                kb.ts(nlam[:], nlam[:], -lam_init, None, op0=ALU.add)
                gsub = kb.sbuf(s2, 'gsub', [128, 64], F32)
                kb.dma(gsub[:], dram_bc(I['diff_subln'][l], 128))
                kb.ts(gsub[:], gsub[:], 1.0 - lam_init, None, op0=ALU.mult)
                a12 = kb.sbuf(s2, 'a12', [128, 2, 4, 64], F32)
                osq = kb.sbuf(s2, 'osq', [128, 64], F32)
                oss = kb.sbuf(s2, 'oss', [128, 1], F32)
                for b in range(NB):
                    c0 = b * TPB
                    for hh in range(4):
                        self.prep_rope(KT[:, hh, :], self.PT[R_DK + hh * 64:R_DK + (hh + 1) * 64, c0:c0 + TPB],
                                       self.PT[R_DKR + hh * 64:R_DKR + (hh + 1) * 64, c0:c0 + TPB],
                                       rope[:, 4, :], rope[:, 5, :], False, tmp)
                    self.load_vx(VX, vst, c0, 18, 384, 4)
                    for hh in range(4):
                        self.prep_rope(QT[:], self.PT[R_DQ + hh * 64:R_DQ + (hh + 1) * 64, c0:c0 + TPB],
                                       self.PT[R_DQR + hh * 64:R_DQR + (hh + 1) * 64, c0:c0 + TPB],
                                       rope[:, 2, :], rope[:, 3, :], False, tmp)
                        blocks = [(CTX + i * 512, 512, 18) for i in range(4)]
                        if need_ctx:
                            blocks.append((0, 256, 2))
                        for bi, (q0, nq, nk) in enumerate(blocks):
                            nsb = nq // 128
                            for pr in range(2):
                                po = self.PS[3 + pr]
                                chunks = [(KT[pr * 32:(pr + 1) * 32, hh, kc * 128:(kc + 1) * 128], VX[:, kc, hh, :], None)
                                          for kc in range(nk)]
                                self.attn_block(QT[pr * 32:(pr + 1) * 32, :], q0, nq, chunks, po[0:65, 0:nq], ebufs, 0)
                                self.attn_fin(po[0:65, 0:nq], nq, oT, [a12[:, pr, j, :] for j in range(nsb)], rc)
                            self.attn_flush()
                            for j in range(nsb):
                                o = a12[:, 0, j, :]
                                kb.stt(o, a12[:, 1, j, :], nlam[:], o, ALU.mult, ALU.add)
                                kb.act(osq[:], o, AF.Square, accum_out=oss[:])
                                kb.ts(oss[:], oss[:], 1.0 / 64, EPS, op0=ALU.mult, op1=ALU.add)
                                kb.act(oss[:], oss[:], AF.Sqrt)
                                kb.recip(oss[:], oss[:])
                                kb.stt(Y[:, q0 // 128 + j, hh * 64:(hh + 1) * 64], o, oss[:], gsub[:], ALU.mult, ALU.mult)
                    if need_ctx:
                        self.tm_to_fm(Y, 18, 768, c0, stg)
                    else:
                        self.tm_to_fm(sb_view(Y[:], 2 * 256, [[256, 16], [1, 256]]), 16, 768, c0 + CTX, stg)
                kb.barrier()

    def fm_to_tm_bf(self, src_fn, nsc, dst_fn):
        kb = self.kb
        for g0 in range(0, nsc, 8):
            g = min(8, nsc - g0)
            pT = self.psb(5 + self._fc % 2)
            self._fc += 1
            for j in range(g):
                kb.tr(pT[:, j * 128:(j + 1) * 128], src_fn(g0 + j), self.ident_b[:])
            kb.copy(dst_fn(g0, g), pT[:, 0:g * 128].rearrange('p (k t) -> p k t', k=g), eng=('act', 'dve')[(g0 // 8) % 2])

    def st_filters(self, l, need_ctx):
        kb, I = self.kb, self.I
        self._fc = 0
        seqs = [('lat', LAT, NF, self.Hl)]
        if need_ctx:
            seqs.append(('ctx', CTX, NFC, self.Hc))
        for nm, n, nf, Hd in seqs:
            nsc = n // 128
            nfc = nf // 128
            with ExitStack() as st:
                zT = kb.sbuf(st, 'hz', [33, n], F32)
                kb.dma(zT[:], I['hz_' + nm])
                w1 = kb.sbuf(st, 'hw1', [33, 64], F32)
                w2 = kb.sbuf(st, 'hw2', [64, 64], F32)
                w3 = kb.sbuf(st, 'hw3', [64, 64], F32)
                w4 = kb.sbuf(st, 'hw4', [64, 512], F32)
                bf = kb.sbuf(st, 'hbf', [64, 6], F32)
                kb.dma(w1[:], I['hy_w1'][l])
                kb.dma(w2[:], I['hy_w2'][l])
                kb.dma(w3[:], I['hy_w3'][l])
                kb.dma(w4[:], I['hy_w4'][l])
                kb.dma(bf[:], I['hy_bf'][l])
                ha = kb.sbuf(st, 'ha', [64, n], F32)
                hb = kb.sbuf(st, 'hb', [64, n], F32)
                tm = kb.sbuf(st, 'htm', [64, n], F32)
                hb2 = kb.sbuf(st, 'hb2', [64, n], F32)
                hb3 = kb.sbuf(st, 'hb3', [64, n], F32)
                bs = min(512, n)

                def layer(inp, K, w, outp, j):
                    for c0 in range(0, n, bs):
                        ps = self.PS[(c0 // bs) % 2]
                        kb.mm(ps[0:64, 0:bs], w[0:K, :], inp[0:K, c0:c0 + bs])
                        kb.ts(tm[:, c0:c0 + bs], ps[0:64, 0:bs], bf[:, j:j + 1], bf[:, 3 + j:4 + j], op0=ALU.add, op1=ALU.mult)
                    for _ in range(2):
                        kb.ts(hb2[:], tm[:], math.pi, -2.0 * math.pi, op0=ALU.is_gt, op1=ALU.mult)
                        kb.ts(hb3[:], tm[:], -math.pi, 2.0 * math.pi, op0=ALU.is_lt, op1=ALU.mult)
                        kb.tt(tm[:], tm[:], hb2[:], ALU.add)
                        kb.tt(tm[:], tm[:], hb3[:], ALU.add)
                    kb.act(outp[:], tm[:], AF.Sin)

                layer(zT, 33, w1, ha, 0)
                layer(ha, 64, w2, hb, 1)
                layer(hb, 64, w3, ha, 2)
                hh = kb.sbuf(st, 'hh', [128, 4, n], F32)
                win = kb.sbuf(st, 'hwin', [128, 4, n], F32)
                kb.dma(win[:], I['hwin_' + nm])
                for cc in range(4):
                    for c0 in range(0, n, bs):
                        ps = self.PS[2 + (c0 // bs) % 2]
                        kb.mm(ps[:, 0:bs], w4[:, cc * 128:(cc + 1) * 128], ha[:, c0:c0 + bs])
                        kb.tt(hh[:, cc, c0:c0 + bs], ps[:, 0:bs], win[:, cc, c0:c0 + bs], ALU.mult)
                nrm = kb.sbuf(st, 'hnrm', [128, 4], F32)
                kb.reduce(nrm[:], hh[:], ALU.add, AX.X, absval=True)
                kb.recip(nrm[:], nrm[:])
                for cc in range(4):
                    kb.ts(hh[:, cc, :], hh[:, cc, :], nrm[:, cc:cc + 1], None, op0=ALU.mult)
                hsd = kb.sbuf(st, 'hsd', [128, 2, 2, n], BF16)
                kb.tt(hsd[:, 0, :, :], hh[:, 0:2, :], hh[:, 2:4, :], ALU.add)
                kb.tt(hsd[:, 1, :, :], hh[:, 2:4, :], hh[:, 0:2, :], ALU.subtract)
                hT = kb.sbuf(st, 'hT', [128, nsc, 2, 256], BF16)
                for kind in range(2):
                    for cc in range(2):
                        self.fm_to_tm_bf(lambda sc: hsd[:, kind, cc, sc * 128:(sc + 1) * 128], nsc,
                                         lambda g0, g: hT[:, g0:g0 + g, kind, cc * 128:(cc + 1) * 128])
                wg = kb.sbuf(st, 'hwg', [128, nfc], F32)
                kb.dma(wg[:], I['wgt_' + nm])
                Hs = kb.sbuf(st, 'Hs', [128, nfc, 2, 256], F32)
                cfs = [kb.sbuf(st, 'cf%d' % i, [128, nsc, 128], BF16) for i in range(2)]
                sfs = [kb.sbuf(st, 'sf%d' % i, [128, nsc, 128], BF16) for i in range(2)]
                for fc in range(nfc):
                    cf, sf = cfs[fc % 2], sfs[fc % 2]
                    kb.dma(cf[:], I['cfwd_' + nm][fc], q='sp')
                    kb.dma(sf[:], I['sfwd_' + nm][fc], q='pool')
                    pr = self.PS[(2 * fc) % 4]
                    pi = self.PS[(2 * fc + 1) % 4]
                    for sc in range(nsc):
                        kb.mm(pr[:, 0:256], cf[:, sc, :], hT[:, sc, 0, :], start=(sc == 0), stop=(sc == nsc - 1))
                    for sc in range(nsc):
                        kb.mm(pi[:, 0:256], sf[:, sc, :], hT[:, sc, 1, :], start=(sc == 0), stop=(sc == nsc - 1))
                    kb.ts(Hs[:, fc, 0, :], pr[:, 0:256], wg[:, fc:fc + 1], None, op0=ALU.mult)
                    kb.ts(Hs[:, fc, 1, :], pi[:, 0:256], wg[:, fc:fc + 1], None, op0=ALU.mult)
                kb.dma(Hd, Hs[:])
                kb.barrier()

    def st_hyena(self, l, need_ctx):
        kb, I = self.kb, self.I
        self._fc = 0
        seqs = [('lat', LAT, NF, self.Hl, CTX)]
        if need_ctx:
            seqs.append(('ctx', CTX, NFC, self.Hc, 0))
        for nm, n, nf, Hd, toff in seqs:
            nsc = n // 128
            nfc = nf // 128
            bs = min(512, n)
            for b in range(NB):
                c0 = b * TPB + toff
                with ExitStack() as st:
                    sw = kb.sbuf(st, 'sw', [128, 6, 4], F32)
                    kb.dma(sw[:], I['hy_sw'][l])
                    hbias = kb.sbuf(st, 'hbias', [128, 2], F32)
                    kb.dma(hbias[:], I['hy_bias'][l])
                    uc = kb.sbuf(st, 'uc', [128, 6, n], F32)
                    zb = kb.sbuf(st, 'zb', [128, 2, n], BF16)
                    zT = kb.sbuf(st, 'zT', [128, nsc, 256], BF16)
                    PQ = kb.sbuf(st, 'PQ', [128, nfc, 2, 256], BF16)
                    with ExitStack() as s2:
                        us = [kb.sbuf(s2, 'u%d' % i, [128, n], F32) for i in range(2)]
                        for cc in range(6):
                            u = us[cc % 2]
                            kb.dma(u[:], self.PT[R_HY + cc * 128:R_HY + (cc + 1) * 128, c0:c0 + n], q=('sp', 'pool')[cc % 2])
                            kb.ts(uc[:, cc, :], u[:], sw[:, cc, 1:2], sw[:, cc, 3:4], op0=ALU.mult, op1=ALU.add)
                            kb.stt(uc[:, cc, 1:n], u[:, 0:n - 1], sw[:, cc, 0:1], uc[:, cc, 1:n], ALU.mult, ALU.add)
                            kb.stt(uc[:, cc, 0:n - 1], u[:, 1:n], sw[:, cc, 2:3], uc[:, cc, 0:n - 1], ALU.mult, ALU.add)
                        kb.tt(uc[:, 2:4, :], uc[:, 2:4, :], uc[:, 4:6, :], ALU.mult)
                        kb.copy(zb[:], uc[:, 2:4, :], eng='act')
                        for cc in range(2):
                            self.fm_to_tm_bf(lambda sc: zb[:, cc, sc * 128:(sc + 1) * 128], nsc,
                                             lambda g0, g: zT[:, g0:g0 + g, cc * 128:(cc + 1) * 128])
                        cfs = [kb.sbuf(s2, 'cf%d' % i, [128, nsc, 128], BF16) for i in range(2)]
                        sfs = [kb.sbuf(s2, 'sf%d' % i, [128, nsc, 128], BF16) for i in range(2)]
                        Hts = [kb.sbuf(s2, 'Ht%d' % i, [128, 2, 256], F32) for i in range(2)]
                        t1 = kb.sbuf(s2, 't1', [128, 256], F32)
                        t2 = kb.sbuf(s2, 't2', [128, 256], F32)
                        for fc in range(nfc):
                            cf, sf, Ht = cfs[fc % 2], sfs[fc % 2], Hts[fc % 2]
                            kb.dma(cf[:], I['cfwd_' + nm][fc], q='sp')
                            kb.dma(sf[:], I['sfwd_' + nm][fc], q='pool')
                            kb.dma(Ht[:], Hd[:, fc, :, :], q='sp')
                            pr = self.PS[(2 * fc) % 4]
                            pi = self.PS[(2 * fc + 1) % 4]
                            for sc in range(nsc):
                                kb.mm(pr[:, 0:256], cf[:, sc, :], zT[:, sc, :], start=(sc == 0), stop=(sc == nsc - 1))
                            for sc in range(nsc):
                                kb.mm(pi[:, 0:256], sf[:, sc, :], zT[:, sc, :], start=(sc == 0), stop=(sc == nsc - 1))
                            kb.tt(t1[:], pr[:, 0:256], Ht[:, 0, :], ALU.mult)
                            kb.tt(t2[:], pi[:, 0:256], Ht[:, 1, :], ALU.mult)
                            kb.tt(PQ[:, fc, 0, :], t1[:], t2[:], ALU.add)
                            kb.tt(t1[:], pi[:, 0:256], Ht[:, 0, :], ALU.mult)
                            kb.tt(t2[:], pr[:, 0:256], Ht[:, 1, :], ALU.mult)
                            kb.tt(PQ[:, fc, 1, :], t1[:], t2[:], ALU.subtract)
                        kb.barrier()
                    with ExitStack() as s2:
                        crs = kb.sbuf(s2, 'cr', [128, nfc, bs], BF16)
                        srs = kb.sbuf(s2, 'sr', [128, nfc, bs], BF16)
                        yb = kb.sbuf(s2, 'yb', [128, 2, n], BF16)
                        t3 = kb.sbuf(s2, 't3', [128, bs], F32)
                        for t0 in range(0, n, bs):
                            kb.dma(crs[:], I['crow_' + nm][:, t0:t0 + bs].rearrange('(fc p) t -> p fc t', p=128), q='sp')
                            kb.dma(srs[:], I['srow_' + nm][:, t0:t0 + bs].rearrange('(fc p) t -> p fc t', p=128), q='pool')
                            for cc in range(2):
                                ps = self.PS[cc]
                                for fc in range(nfc):
                                    kb.mm(ps[:, 0:bs], PQ[:, fc, 0, cc * 128:(cc + 1) * 128], crs[:, fc, :], start=(fc == 0), stop=False)
                                    kb.mm(ps[:, 0:bs], PQ[:, fc, 1, cc * 128:(cc + 1) * 128], srs[:, fc, :], start=False, stop=(fc == nfc - 1))
                                kb.stt(t3[:, 0:bs], uc[:, 2 + cc, t0:t0 + bs], hbias[:, cc:cc + 1], ps[:, 0:bs], ALU.mult, ALU.add)
                                kb.tt(yb[:, cc, t0:t0 + bs], t3[:, 0:bs], uc[:, cc, t0:t0 + bs], ALU.mult)
                        for cc in range(2):
                            kb.dma(self.YT[256 + cc * 128:256 + (cc + 1) * 128, c0:c0 + n], yb[:, cc, :], q=('sp', 'pool')[cc])
                        kb.barrier()

    def st_merge(self, l):
        kb, I = self.kb, self.I
        with ExitStack() as st:
            wbr = kb.sbuf(st, 'wbr', [128, 8, D], BF16)
            wo = kb.sbuf(st, 'wo', [128, 8, D], BF16)
            wst = [kb.sbuf(st, 'wst%d' % i, [128, D], F32) for i in range(2)]
            wbv = I['w_branch'][l].rearrange('k (wc p) d -> p (k wc) d', p=128)
            wov = I['w_out'][l].rearrange('(kc p) d -> p kc d', p=128)
            for j in range(8):
                kb.dma(wst[0][:], wbv[:, j, :], q='sp')
                kb.copy(wbr[:, j, :], wst[0][:], eng='dve')
                kb.dma(wst[1][:], wov[:, j, :], q='pool')
                kb.copy(wo[:, j, :], wst[1][:], eng='act')
            G1 = self.load_bc(st, 'G1', 2048)
            yT = kb.sbuf(st, 'yT', [128, 8, 512], BF16)
            mixT = kb.sbuf(st, 'mixT', [128, 8, 512], BF16)
            gls = [kb.sbuf(st, 'gl%d' % i, [128, 512], F32) for i in range(3)]
            sgs = [kb.sbuf(st, 'sg%d' % i, [128, 512], F32) for i in range(2)]
            acc = kb.sbuf(st, 'macc', [128, 512], F32)
            tmps = [kb.sbuf(st, 'mtmp%d' % i, [128, 512], F32) for i in range(2)]
            hts = [kb.sbuf(st, 'ht%d' % i, [128, D], F32) for i in range(2)]
            cnt = 0
            tgen = self.peer_tables_gen(l, st) if self.want('peer') else iter(())
            for blk in range(T // 512):
                t0 = blk * 512
                kb.dma(yT[:], self.YT[:, t0:t0 + 512].rearrange('(j p) t -> p j t', p=128))
                for dc in range(8):
                    for _ in range(2):
                        next(tgen, None)
                    for k in range(4):
                        ps = self.PS[cnt % 3]
                        gl = gls[cnt % 3]
                        sg = sgs[cnt % 2]
                        tmp = tmps[cnt % 2]
                        cnt += 1
                        for wc in range(2):
                            kb.mm(ps[:, :], wbr[:, k * 2 + wc, dc * 128:(dc + 1) * 128], yT[:, k * 2 + wc, :],
                                  start=(wc == 0), stop=(wc == 1))
                        row = R_GATE + k * 1024 + dc * 128
                        kb.dma(gl[:], self.PT[row:row + 128, t0:t0 + 512], q=('sp', 'pool')[cnt % 2])
                        kb.act(sg[:], gl[:], AF.Sigmoid)
                        if k == 0:
                            kb.tt(acc[:], ps[:, :], sg[:], ALU.mult)
                        elif k < 3:
                            kb.tt(tmp[:], ps[:, :], sg[:], ALU.mult)
                            kb.tt(acc[:], acc[:], tmp[:], ALU.add, eng='pool')
                        else:
                            kb.tt(tmp[:], ps[:, :], sg[:], ALU.mult)
                            kb.tt(mixT[:, dc, :], acc[:], tmp[:], ALU.add, eng='pool')
                for ti in range(4):
                    i = blk * 4 + ti
                    r = row_type(i)
                    ht = hts[i % 2]
                    kb.dma(ht[:], self.h[i * 128:(i + 1) * 128, :], q='sp')
                    for half in range(2):
                        ps = self.PS[4 + half + 2 * (i % 2)]
                        for dc in range(8):
                            kb.mm(ps[:, :], mixT[:, dc, ti * 128:(ti + 1) * 128], wo[:, dc, half * 512:(half + 1) * 512],
                                  start=(dc == 0), stop=(dc == 7))
                        tmp = tmps[half]
                        kb.tt(tmp[:], ps[:, :], G1[:, r, half * 512:(half + 1) * 512], ALU.mult)
                        kb.tt(ht[:, half * 512:(half + 1) * 512], ht[:, half * 512:(half + 1) * 512], tmp[:], ALU.add)
                    kb.dma(self.h[i * 128:(i + 1) * 128, :], ht[:], q='pool')
            for _ in tgen:
                pass
            kb.barrier()

    def peer_tables_gen(self, l, st):
        kb, I = self.kb, self.I
        u32 = [kb.sbuf(st, 'u32%d' % i, [128, D], F32) for i in range(2)]
        ub = [kb.sbuf(st, 'ub%d' % i, [128, D], BF16) for i in range(2)]
        uT = [kb.sbuf(st, 'uT%d' % i, [128, 8, 128], BF16) for i in range(2)]
        v32 = [kb.sbuf(st, 'v32%d' % i, [128, D], F32) for i in range(2)]
        vb = [kb.sbuf(st, 'vb%d' % i, [128, D], BF16) for i in range(2)]
        for ec in range(128):
            k = ec % 2
            kb.dma(u32[k][:], I['peer_u'][l, ec * 128:(ec + 1) * 128, :], q='sp')
            kb.copy(ub[k][:], u32[k][:], eng='dve')
            pT = self.psb(3)
            for kc in range(8):
                kb.tr(pT[:, kc * 128:(kc + 1) * 128], ub[k][:, kc * 128:(kc + 1) * 128], self.ident_b[:])
            kb.copy(uT[k][:], pT.rearrange('p (k e) -> p k e', k=8), eng='act')
            kb.dma(self.UT[:, :, ec * 128:(ec + 1) * 128], uT[k][:], q='sp')
            kb.dma(v32[k][:], I['peer_v'][l, ec * 128:(ec + 1) * 128, :], q='pool')
            kb.copy(vb[k][:], v32[k][:], eng='pool')
            kb.dma(self.VB[ec * 128:(ec + 1) * 128, :], vb[k][:], q='pool')
            yield

    def st_peer(self, l, need_ctx):
        kb, I = self.kb, self.I
        tiles = [i for i in range(NT) if need_ctx or (i % 18) >= 2]
        GT = 4
        groups = [tiles[i:i + GT] for i in range(0, len(tiles), GT)]
        with ExitStack() as st:
            wq = kb.sbuf(st, 'wq', [128, 8, D], BF16)
            kbk = kb.sbuf(st, 'kbk', [128, 8, 256], BF16)
            with ExitStack() as s0:
                wst = [kb.sbuf(s0, 'pwst%d' % i, [128, D], F32) for i in range(2)]
                wqv = I['peer_wq'][l].rearrange('(kc p) d -> p kc d', p=128)
                for j in range(8):
                    kb.dma(wst[j % 2][:], wqv[:, j, :], q=('sp', 'pool')[j % 2])
                    kb.copy(wq[:, j, :], wst[j % 2][:], eng=('dve', 'act')[j % 2])
                k32 = kb.sbuf(s0, 'k32', [128, 8, 256], F32)
                kb.dma(k32[:], I['peer_kb'][l])
                kb.copy(kbk[:], k32[:])
                kb.barrier()
            mT = kb.sbuf(st, 'mT', [128, 8, GT * 128], BF16)
            S = [kb.sbuf(st, 'S%d' % i, [128, 2048], F32) for i in range(GT)]
            negb = kb.sbuf(st, 'negb', [128, GT, 8], F32)
            thr = kb.sbuf(st, 'thr', [128, GT, 8], F32)
            oacc = [kb.sbuf(st, 'oacc%d' % i, [128, D], F32) for i in range(GT)]
            Abc = kb.sbuf(st, 'Abc', [128, D], F32)
            Bbc = kb.sbuf(st, 'Bbc', [128, D], F32)
            x = kb.sbuf(st, 'px', [128, D], F32)
            x2 = kb.sbuf(st, 'px2', [128, D], F32)
            for grp in groups:
                nt = len(grp)
                ntok = nt * 128
                with ExitStack() as s1:
                    junk = kb.sbuf(s1, 'pjunk', [128, D], BF16)
                    qT = kb.sbuf(s1, 'qT', [128, 8, GT * 128], BF16)
                    mb = kb.sbuf(s1, 'pmb', [128, D], BF16)
                    ss = kb.sbuf(s1, 'pss', [128, 1], F32)
                    sv = kb.sbuf(s1, 'sv', [128, 16, 16], F32)
                    stmp = kb.sbuf(s1, 'stmp', [128, 2048], F32)
                    cand = kb.sbuf(s1, 'cand', [128, 8, 256], F32)
                    ctmp = kb.sbuf(s1, 'ctmp', [128, 8, 256], F32)
                    best = kb.sbuf(s1, 'best', [128, 8, 16], F32)
                    eb = kb.sbuf(s1, 'eb', [128, 8, 16], F32)
                    zs = kb.sbuf(s1, 'zs', [128, 8], F32)
                    for gi, i in enumerate(grp):
                        r = row_type(i)
                        kb.dma(Abc[:], dram_bc(self.modx[r, 4096:5120], 128), q='pool')
                        kb.dma(Bbc[:], dram_bc(self.modx[r, 3072:4096], 128), q='pool')
                        self.norm_tile(x, x2, junk, ss, self.h[i * 128:(i + 1) * 128, :], Abc[:], Bbc[:], mb[:])
                        pT = self.psb(gi % 2)
                        for kc in range(8):
                            kb.tr(pT[:, kc * 128:(kc + 1) * 128], mb[:, kc * 128:(kc + 1) * 128], self.ident_b[:])
                        kb.copy(mT[:, :, gi * 128:(gi + 1) * 128], pT.rearrange('p (k t) -> p k t', k=8), eng='act')
                    for hc in range(8):
                        ps = self.PS[2 + hc % 2]
                        for kc in range(8):
                            kb.mm(ps[:, 0:ntok], wq[:, kc, hc * 128:(hc + 1) * 128], mT[:, kc, 0:ntok],
                                  start=(kc == 0), stop=(kc == 7))
                        kb.copy(qT[:, hc, 0:ntok], ps[:, 0:ntok], eng=('act', 'dve')[hc % 2])
                    for gi in range(nt):
                        Sg = S[gi]
                        for hp in range(4):
                            ps = self.PS[4 + hp % 2]
                            for j in range(2):
                                hh = 2 * hp + j
                                kb.mm(ps[:, j * 256:(j + 1) * 256], qT[:, hh, gi * 128:(gi + 1) * 128], kbk[:, hh, :])
                            kb.copy(Sg[:, hp * 512:(hp + 1) * 512], ps[:, :], eng='act')
                        for g in range(16):
                            sl = slice(g * 128, (g + 1) * 128)
                            kb.max8(sv[:, g, 0:8], Sg[:, sl])
                            kb.match_replace(stmp[:, sl], sv[:, g, 0:8], Sg[:, sl], -1e30)
                            kb.max8(sv[:, g, 8:16], stmp[:, sl])
                        kb.tt(sb_view(cand[:], 0, [[256, 8], [16, 16], [1, 16]]),
                              sb_view(sv[:], 0, [[32, 8], [1, 16], [0, 16]]),
                              sb_view(sv[:], 16, [[32, 8], [0, 16], [1, 16]]), ALU.add)
                        for hh in range(8):
                            kb.max8(best[:, hh, 0:8], cand[:, hh, :])
                            kb.match_replace(ctmp[:, hh, :], best[:, hh, 0:8], cand[:, hh, :], -1e30)
                            kb.max8(best[:, hh, 8:16], ctmp[:, hh, :])
                        kb.tt(eb[:], best[:], sb_view(best[:], 0, [[16, 8], [0, 16]]), ALU.subtract)
                        kb.act(eb[:], eb[:], AF.Exp)
                        kb.reduce(zs[:], eb[:], ALU.add, AX.X)
                        kb.act(zs[:], zs[:], AF.Ln)
                        kb.tt(negb[:, gi, :], zs[:], sb_view(best[:], 0, [[16, 8]]), ALU.add)
                        kb.ts(negb[:, gi, :], negb[:, gi, :], -1.0, None, op0=ALU.mult)
                        kb.copy(thr[:, gi, :], sb_view(best[:], 15, [[16, 8]]))
                    kb.barrier()
                with ExitStack() as s2:
                    ut = [kb.sbuf(s2, 'ut%d' % i, [128, 8, 1024], BF16) for i in range(2)]
                    vt = [kb.sbuf(s2, 'vt%d' % i, [128, 8, 1024], BF16) for i in range(2)]
                    NBF = 4
                    sm = [kb.sbuf(s2, 'sm%d' % i, [128, 1024], F32) for i in range(NBF)]
                    ee = [kb.sbuf(s2, 'ee%d' % i, [128, 1024], BF16) for i in range(NBF)]
                    gm = [kb.sbuf(s2, 'gm%d' % i, [128, 1024], BF16) for i in range(NBF)]
                    ga = [kb.sbuf(s2, 'ga%d' % i, [128, 1024], BF16) for i in range(2)]
                    wt = [kb.sbuf(s2, 'wt%d' % i, [128, 1024], BF16) for i in range(2)]
                    wT = [kb.sbuf(s2, 'wT%d' % i, [128, 8, 128], BF16) for i in range(2)]
                    otmp = [kb.sbuf(s2, 'otmp%d' % i, [128, 512], F32) for i in range(2)]
                    cnt = 0
                    it = 0
                    pend = None
                    iters = [(ec, gi) for ec in range(16) for gi in range(nt)]
                    sums = {}
                    ACT_HEADS = ()
                    LEAD = 3

                    def emit_sum(n, hh):
                        nonlocal cnt
                        ec_, gi_ = iters[n]
                        Sg_ = S[gi_]
                        k = cnt % NBF
                        cnt += 1
                        if hh in ACT_HEADS:
                            for ii in range(8):
                                kb.act(sm[k][:, ii * 128:(ii + 1) * 128], Sg_[:, hh * 256 + 128:hh * 256 + 256], AF.Identity,
                                       bias=Sg_[:, hh * 256 + ec_ * 8 + ii:hh * 256 + ec_ * 8 + ii + 1], scale=1.0)
                        else:
                            kb.tt(sb_view(sm[k][:], 0, [[128, 8], [1, 128]]),
                                  sb_view(Sg_[:], hh * 256 + ec_ * 8, [[1, 8], [0, 128]]),
                                  sb_view(Sg_[:], hh * 256 + 128, [[0, 8], [1, 128]]), ALU.add, eng='dve')
                        sums[(n, hh)] = k

                    for h0 in range(LEAD):
                        emit_sum(0, h0)
                    for n, (ec, gi) in enumerate(iters):
                        if gi == 0:
                            u_t, v_t = ut[ec % 2], vt[ec % 2]
                            kb.dma(u_t[:], self.UT[:, :, ec * 1024:(ec + 1) * 1024], q='sp')
                            kb.dma(v_t[:], self.VB[ec * 1024:(ec + 1) * 1024, :].rearrange('(j p) d -> p j d', p=128), q='pool')
                        if True:
                            Sg = S[gi]
                            k2 = it % 2
                            it += 1
                            for half in range(2):
                                for kc in range(8):
                                    kb.mm(self.PS[half][:, :], mT[:, kc, gi * 128:(gi + 1) * 128],
                                          u_t[:, kc, half * 512:(half + 1) * 512], start=(kc == 0), stop=(kc == 7))
                            for hh in range(8):
                                k = sums.pop((n, hh))
                                kb.act(ee[k][:], sm[k][:], AF.Exp, bias=negb[:, gi, hh:hh + 1])
                                if hh + LEAD < 8:
                                    emit_sum(n, hh + LEAD)
                                elif n + 1 < len(iters):
                                    emit_sum(n + 1, hh + LEAD - 8)
                                kb.stt(gm[k][:], sm[k][:], thr[:, gi, hh:hh + 1], ee[k][:], ALU.is_ge, ALU.mult)
                                for half in range(2):
                                    kb.mm(self.PS[2 + half][:, :], self.ident_b[:], gm[k][:, half * 512:(half + 1) * 512],
                                          start=(hh == 0), stop=(hh == 7))
                                if hh == 3:
                                    for half in range(2):
                                        kb.act(ga[k2][:, half * 512:(half + 1) * 512], self.PS[half][:, :], AF.Gelu)
                            for half in range(2):
                                kb.tt(wt[k2][:, half * 512:(half + 1) * 512], self.PS[2 + half][:, :],
                                      ga[k2][:, half * 512:(half + 1) * 512], ALU.mult)
                            if pend is not None:
                                pend()

                            def tail(k2=k2, v_t=v_t, gi=gi, ec=ec):
                                pT = self.psb(6)
                                for j in range(8):
                                    kb.tr(pT[:, j * 128:(j + 1) * 128], wt[k2][:, j * 128:(j + 1) * 128], self.ident_b[:])
                                kb.copy(wT[k2][:], pT.rearrange('p (j t) -> p j t', j=8), eng='act')
                                for half in range(2):
                                    po = self.PS[4 + half]
                                    for j in range(8):
                                        kb.mm(po[:, :], wT[k2][:, j, :], v_t[:, j, half * 512:(half + 1) * 512],
                                              start=(j == 0), stop=(j == 7))
                                    oa = oacc[gi][:, half * 512:(half + 1) * 512]
                                    if ec == 0:
                                        kb.copy(oa, po[:, :], eng='act')
                                    else:
                                        ot = otmp[half]
                                        kb.copy(ot[:], po[:, :], eng='act')
                                        kb.tt(oa, oa, ot[:], ALU.add, eng='pool')
                            pend = tail
                    if pend is not None:
                        pend()
                        pend = None
                    kb.barrier()
                for gi, i in enumerate(grp):
                    r = row_type(i)
                    kb.dma(Abc[:], dram_bc(self.modx[r, 5120:6144], 128), q='pool')
                    kb.dma(x[:], self.h[i * 128:(i + 1) * 128, :], q='sp')
                    kb.tt(x2[:], oacc[gi][:], Abc[:], ALU.mult)
                    kb.tt(x[:], x[:], x2[:], ALU.add)
                    kb.dma(self.h[i * 128:(i + 1) * 128, :], x[:], q='sp')
            kb.barrier()

    def st_final(self):
        kb, I = self.kb, self.I
        with ExitStack() as st:
            g = kb.sbuf(st, 'fg', [128, D], F32)
            kb.dma(g[:], dram_bc(I['final_g'].rearrange('a d -> (a d)'), 128))
            xs = [kb.sbuf(st, 'fx%d' % i, [128, D], F32) for i in range(2)]
            ys = [kb.sbuf(st, 'fy%d' % i, [128, D], F32) for i in range(2)]
            junk = kb.sbuf(st, 'fjunk', [128, D], BF16)
            sss = [kb.sbuf(st, 'fss%d' % i, [128, 1], F32) for i in range(2)]
            k = 0
            for b in range(NB):
                for j in range(LAT // 128):
                    i = b * 18 + 2 + j
                    x, y, ss = xs[k % 2], ys[k % 2], sss[k % 2]
                    kb.dma(x[:], self.h[i * 128:(i + 1) * 128, :], q='sp')
                    kb.act(junk[:], x[:], AF.Square, accum_out=ss[:])
                    kb.ts(ss[:], ss[:], 1.0 / D, EPS, op0=ALU.mult, op1=ALU.add)
                    kb.act(ss[:], ss[:], AF.Sqrt)
                    kb.recip(ss[:], ss[:])
                    kb.stt(y[:], x[:], ss[:], g[:], ALU.mult, ALU.mult)
                    kb.dma(self.out[b, j * 128:(j + 1) * 128, :], y[:], q='pool')
                    k += 1
            kb.barrier()


def core_inputs(inp, L, C, core):
    b0 = core * NB
    d = {}
    d['h0'] = np.ascontiguousarray(np.concatenate(
        [np.concatenate([inp['ctx'][b], inp['x'][b]], axis=0) for b in range(b0, b0 + NB)], axis=0))
    crow = np.stack([inp['c'][b0], inp['c'][b0 + 1], inp['c_ctx']])
    d['cT'] = np.ascontiguousarray(crow.reshape(3, 8, 128).transpose(2, 1, 0))
    d.update(L)
    d.update(C)
    return d


def in_shapes(d):
    sh = {}
    for k, v in d.items():
        sh[k] = (v.shape, BF16 if v.dtype == ml_dtypes.bfloat16 else F32)
    return sh


_CACHE = {}


def kernel(**inputs):
    inp = {k: np.asarray(v) for k, v in inputs.items()}
    L = host_layout(inp)
    C = _CACHE.get('consts')
    if C is None:
        C = host_consts()
        _CACHE['consts'] = C
    cis = [core_inputs(inp, L, C, c) for c in range(8)]
    prog = Prog()
    nc = prog.build(in_shapes(cis[0]))
    res = run_bass_kernel_spmd(nc, cis, core_ids=list(range(8)))
    out = np.concatenate([np.asarray(r['out']) for r in res.results], axis=0)
    return np.ascontiguousarray(out.astype(np.float32))
```

```python
import math
from contextlib import ExitStack

import numpy as np
import ml_dtypes

import concourse.bass as bass
import concourse.mybir as mybir
from concourse.bass_utils import run_bass_kernel_spmd

F32 = mybir.dt.float32
BF16 = mybir.dt.bfloat16
I32 = mybir.dt.int32
AF = mybir.ActivationFunctionType
ALU = mybir.AluOpType
AX = mybir.AxisListType

_ESZ = {F32: 4, BF16: 2, I32: 4, mybir.dt.uint32: 4, mybir.dt.float32r: 4,
        mybir.dt.int16: 2, mybir.dt.uint16: 2, mybir.dt.uint8: 1, mybir.dt.int8: 1}


def _prod(xs):
    r = 1
    for x in xs:
        r *= int(x)
    return r


class KB:
    NDS = 8

    def __init__(self, nc, es):
        self.nc = nc
        self.es = es
        self.e = dict(pe=nc.tensor, dve=nc.vector, act=nc.scalar, pool=nc.gpsimd, sp=nc.sync)
        self.sems = {}
        for n in ('pe', 'dve', 'act', 'pool'):
            self.sems[('c', n, 0)] = es.enter_context(nc.semaphore('c_' + n))
        self.ccnt = {n: 0 for n in ('pe', 'dve', 'act', 'pool')}
        self.cgen = {n: 0 for n in ('pe', 'dve', 'act', 'pool')}
        self.cold = {}
        self.dqs = ('sp', 'act', 'pool')
        for q in self.dqs:
            for i in range(self.NDS):
                self.sems[('d', q, i)] = es.enter_context(nc.semaphore('d_%s%d' % (q, i)))
        self.dcnt = {q: 0 for q in self.dqs}
        self.waited = {n: {} for n in self.e}
        self.acc = {}
        self.const = set()
        self.pbytes = {}
        self.n_ins = 0
        self.uid = 0

    def dram(self, name, shape, dtype, kind='Internal'):
        t = self.nc.dram_tensor(name, list(shape), dtype, kind=kind)
        if kind == 'ExternalInput':
            self.const.add(name)
        return t.ap()

    def sbuf(self, st, name, shape, dtype):
        self.uid += 1
        t = st.enter_context(self.nc.sbuf_tensor('%s_%d' % (name, self.uid), list(shape), dtype))
        return t

    def psum(self, st, name, shape, dtype):
        self.uid += 1
        t = st.enter_context(self.nc.psum_tensor('%s_%d' % (name, self.uid), list(shape), dtype))
        return t

    def _box(self, a):
        esz = _ESZ[a.dtype]
        off = int(a.offset) * esz
        aps = a.ap
        space = str(a.space)
        if space == 'DRAM':
            ext = 0
            for st, cnt in aps:
                ext += (cnt - 1) * abs(st)
            return (0, 0, off, off + ext * esz + esz)
        name = a.name
        pb = self.pbytes.get(name)
        if pb is None:
            th = a.tensor
            pb = _prod(th.shape[1:]) * _ESZ[th.dtype]
            self.pbytes[name] = pb
        p0 = off // pb
        f0 = off % pb
        pst, pcnt = aps[0]
        pstep = (pst * esz) // pb if pb else 0
        p1 = p0 + (pcnt - 1) * pstep
        ext = 0
        for st, cnt in aps[1:]:
            ext += (cnt - 1) * abs(st)
        return (p0, p1, f0, f0 + ext * esz + esz)

    @staticmethod
    def _ovl(a, b):
        return not (a[1] < b[0] or b[1] < a[0] or a[3] <= b[2] or b[3] <= a[2])

    @staticmethod
    def _contains(a, b):
        return a[0] <= b[0] and a[1] >= b[1] and a[2] <= b[2] and a[3] >= b[3]

    def _deps(self, reads, writes):
        deps = []
        for a in reads:
            if a.name in self.const:
                continue
            lst = self.acc.get(a.name)
            if not lst:
                continue
            bx = self._box(a)
            for (b, evt, w, eng) in lst:
                if w and self._ovl(bx, b):
                    deps.append(evt)
        for a in writes:
            lst = self.acc.get(a.name)
            if not lst:
                continue
            bx = self._box(a)
            for (b, evt, w, eng) in lst:
                if self._ovl(bx, b):
                    deps.append(evt)
        return deps

    def _record(self, reads, writes, evt, eng):
        for a in writes:
            bx = self._box(a)
            lst = self.acc.setdefault(a.name, [])
            lst[:] = [x for x in lst if not self._contains(bx, x[0])]
            lst.append((bx, evt, True, eng))
        for a in reads:
            if a.name in self.const:
                continue
            bx = self._box(a)
            lst = self.acc.setdefault(a.name, [])
            if evt[0][0] == 'c':
                lst[:] = [x for x in lst if not ((not x[2]) and x[3] == eng and x[1][0][0] == 'c'
                                                 and self._contains(bx, x[0]))]
            lst.append((bx, evt, False, eng))

    def _wait(self, eng, deps):
        w = self.waited[eng]
        best = {}
        for key, val in deps:
            if key[0] == 'c' and key[1] == 'pe' and eng == 'pe':
                continue
            if best.get(key, 0) < val:
                best[key] = val
        for key, val in best.items():
            if w.get(key, 0) >= val:
                continue
            self.e[eng].wait_ge(self.sems[key], val)
            w[key] = val

    def emit(self, eng, fn, reads, writes):
        self._wait(eng, self._deps(reads, writes))
        ins = fn()
        if self.ccnt[eng] >= 60000:
            self.cold[eng] = (('c', eng, self.cgen[eng]), self.ccnt[eng])
            self.cgen[eng] += 1
            self.ccnt[eng] = 0
            self.sems[('c', eng, self.cgen[eng])] = self.es.enter_context(
                self.nc.semaphore('c_%s_%d' % (eng, self.cgen[eng])))
        self.ccnt[eng] += 1
        key = ('c', eng, self.cgen[eng])
        ins.then_inc(self.sems[key], 1)
        self._record(reads, writes, (key, self.ccnt[eng]), eng)
        self.n_ins += 1
        return ins

    def dma(self, out, in_, q='sp', **kw):
        i = self.dcnt[q]
        key = ('d', q, i % self.NDS)
        deps = self._deps([in_], [out])
        if i >= self.NDS:
            deps.append((key, 16 * (i // self.NDS)))
        self._wait(q, deps)
        ins = self.e[q].dma_start(out=out, in_=in_, **kw)
        ins.then_inc(self.sems[key], 16)
        self.dcnt[q] += 1
        self._record([in_], [out], (key, 16 * (i // self.NDS + 1)), q)
        self.n_ins += 1
        return ins

    def _all_events(self):
        evts = []
        for n, c in self.ccnt.items():
            if c > 0:
                evts.append((('c', n, self.cgen[n]), c))
            elif n in self.cold:
                evts.append(self.cold[n])
        return evts

    def barrier(self):
        evts = self._all_events()
        for q in self.dqs:
            n = self.dcnt[q]
            for j in range(self.NDS):
                cnt = (n - j + self.NDS - 1) // self.NDS if n > j else 0
                if cnt > 0:
                    evts.append((('d', q, j), 16 * cnt))
        for eng in self.e:
            self._wait(eng, [e for e in evts if not (e[0][0] == 'c' and e[0][1] == eng)])
        self.acc = {}

    def finish(self):
        evts = self._all_events()
        for q in self.dqs:
            n = self.dcnt[q]
            for j in range(self.NDS):
                cnt = (n - j + self.NDS - 1) // self.NDS if n > j else 0
                if cnt > 0:
                    evts.append((('d', q, j), 16 * cnt))
        self._wait('sp', evts)

    def mm(self, out, lhsT, rhs, start=True, stop=True):
        return self.emit('pe', lambda: self.nc.tensor.matmul(out, lhsT, rhs, start=start, stop=stop),
                         [lhsT, rhs], [out])

    def tr(self, out, in_, ident):
        return self.emit('pe', lambda: self.nc.tensor.transpose(out, in_, ident), [in_, ident], [out])

    def act(self, out, in_, func, bias=None, scale=None, accum_out=None, eng='act'):
        kw = {}
        rd = [in_]
        wr = [out]
        if bias is not None:
            kw['bias'] = bias
            if not isinstance(bias, (int, float)):
                rd.append(bias)
        if scale is not None:
            kw['scale'] = scale
            if not isinstance(scale, (int, float)):
                rd.append(scale)
        if accum_out is not None:
            kw['accum_out'] = accum_out
            wr.append(accum_out)
        return self.emit('act', lambda: self.nc.scalar.activation(out, in_, func, **kw), rd, wr)

    def tt(self, out, in0, in1, op, eng='dve'):
        return self.emit(eng, lambda: self.e[eng].tensor_tensor(out, in0, in1, op), [in0, in1], [out])

    def ts(self, out, in0, s1, s2=None, op0=ALU.mult, op1=None, eng='dve', accum_out=None):
        rd = [in0]
        wr = [out]
        if not isinstance(s1, (int, float)):
            rd.append(s1)
        if s2 is not None and not isinstance(s2, (int, float)):
            rd.append(s2)
        kw = {}
        if op1 is not None:
            kw['op1'] = op1
        if accum_out is not None:
            kw['accum_out'] = accum_out
            wr.append(accum_out)
        return self.emit(eng, lambda: self.e[eng].tensor_scalar(out, in0, s1, s2, op0, **kw), rd, wr)

    def stt(self, out, in0, scalar, in1, op0, op1, accum_out=None):
        rd = [in0, in1]
        wr = [out]
        if not isinstance(scalar, (int, float)):
            rd.append(scalar)
        kw = {}
        if accum_out is not None:
            kw['accum_out'] = accum_out
            wr.append(accum_out)
        return self.emit('dve', lambda: self.nc.vector.scalar_tensor_tensor(out, in0, scalar, in1, op0, op1, **kw),
                         rd, wr)

    def copy(self, out, in_, eng='dve'):
        if eng == 'act':
            return self.emit('act', lambda: self.nc.scalar.copy(out, in_), [in_], [out])
        return self.emit(eng, lambda: self.e[eng].tensor_copy(out, in_), [in_], [out])

    def memset(self, out, val, eng='dve'):
        return self.emit(eng, lambda: self.e[eng].memset(out, val), [], [out])

    def reduce(self, out, in_, op=ALU.add, axis=AX.X, absval=None, eng='dve'):
        kw = {}
        if absval:
            kw['apply_absolute_value'] = True
        return self.emit(eng, lambda: self.e[eng].tensor_reduce(out, in_, axis, op, **kw), [in_], [out])

    def recip(self, out, in_):
        return self.emit('dve', lambda: self.nc.vector.reciprocal(out, in_), [in_], [out])

    def max8(self, out, in_):
        return self.emit('dve', lambda: self.nc.vector.max(out, in_), [in_], [out])

    def match_replace(self, out, to_replace, values, imm):
        return self.emit('dve', lambda: self.nc.vector.match_replace(out, to_replace, values, imm),
                         [to_replace, values], [out])


DEPTH = 4
D = 1024
NB = 2
CTX = 256
LAT = 2048
TPB = CTX + LAT
T = NB * TPB
NT = T // 128
GRID_W = 64
EPS = 1e-6
NEG = -30000.0
R_NAQ, R_NAK, R_HY, R_GQ, R_GQR, R_GK, R_GKR, R_DQ, R_DQR, R_DK, R_DKR, R_GATE = (
    0, 256, 512, 1280, 1536, 1792, 1920, 2048, 2304, 2560, 2816, 3072)
NFM = 7168
NTM = 640
INW2 = NFM + NTM
NF = 2176
NFC = 384


def _perm(n_total, group):
    half = group // 2
    idx = np.arange(n_total)
    return np.where((idx % group) < half, idx + half, idx - half)


def host_consts():
    c = {}
    c['ident'] = np.eye(128, dtype=np.float32)
    t = np.arange(LAT)
    rows = (t // GRID_W).astype(np.float64)
    cols = (t % GRID_W).astype(np.float64)

    def rope_tab(dh):
        h = dh // 2
        half = h // 2
        f = np.arange(dh)
        pos = np.where(f[:, None] < h, rows[None, :], cols[None, :])
        idx = (f % h) % half
        freq = 10000.0 ** (-idx.astype(np.float64) / half)
        ang = pos * freq[:, None]
        cs = np.cos(ang)
        sn = np.sin(ang)
        first = ((f % h) < half)
        sn = np.where(first[:, None], -sn, sn)
        cs_full = np.concatenate([np.ones((dh, CTX)), cs], axis=1)
        sn_full = np.concatenate([np.zeros((dh, CTX)), sn], axis=1)
        return cs_full, sn_full

    c64, s64 = rope_tab(64)
    c32, s32 = rope_tab(32)
    c32 = np.concatenate([c32, c32], axis=0)
    s32 = np.concatenate([s32, s32], axis=0)
    sc = 32.0 ** -0.5
    c['rope'] = np.stack([c64, s64, c32 * sc, s32 * sc, c32, s32]).astype(np.float32)
    for nm, n in (('lat', LAT), ('ctx', CTX)):
        tt_ = np.linspace(0.0, 1.0, n, dtype=np.float32)[:, None]
        w = (np.float32(2.0 * math.pi / n) * np.arange(n, dtype=np.float32))[:, None]
        bands = np.linspace(1e-4, 16 - 1, 16, dtype=np.float32)[None, :]
        z = np.concatenate([tt_, np.cos(w * bands), np.sin(w * bands)], axis=-1).astype(np.float32)
        c['hz_' + nm] = np.ascontiguousarray(z.T)
        deltas = np.linspace(math.log(1e-2) / 1.5, math.log(1e-2) / 0.3, 256, dtype=np.float32)
        deltas = np.tile(np.abs(deltas), 2)
        win = np.exp(-tt_ * deltas[None, :]).astype(np.float32)
        c['hwin_' + nm] = np.ascontiguousarray(win.T.reshape(4, 128, n).transpose(1, 0, 2))
    for nm, n, nf in (('lat', LAT, NF), ('ctx', CTX, NFC)):
        N = 2 * n
        a = np.arange(nf)[:, None].astype(np.float64)
        b = np.arange(n)[None, :].astype(np.float64)
        ang = 2.0 * math.pi * ((a * b) % N) / N
        Cr = np.cos(ang)
        Sr = np.sin(ang)
        c['crow_' + nm] = Cr.astype(ml_dtypes.bfloat16)
        c['srow_' + nm] = Sr.astype(ml_dtypes.bfloat16)
        nsc = n // 128
        nfc = nf // 128
        Cf = Cr.T.reshape(nsc, 128, nfc, 128).transpose(2, 1, 0, 3)
        Sf = Sr.T.reshape(nsc, 128, nfc, 128).transpose(2, 1, 0, 3)
        c['cfwd_' + nm] = np.ascontiguousarray(Cf).astype(ml_dtypes.bfloat16)
        c['sfwd_' + nm] = np.ascontiguousarray(Sf).astype(ml_dtypes.bfloat16)
        wg = np.full((nf,), 2.0 / N)
        wg[0] = 1.0 / N
        wg[n] = 1.0 / N
        wg[n + 1:] = 0.0
        c['wgt_' + nm] = np.ascontiguousarray(wg.reshape(nfc, 128).T).astype(np.float32)
    return c


def host_layout(inp):
    L = {}
    w_in = inp['w_in']
    p64q = _perm(256, 32)
    p64k = _perm(128, 32)
    p32 = _perm(256, 16)
    cols = np.concatenate([
        np.arange(0, 256), np.arange(256, 512), np.arange(768, 1536),
        1536 + np.arange(256), 1536 + p64q, 1792 + np.arange(128), 1792 + p64k,
        2048 + np.arange(256), 2048 + p32, 2304 + np.arange(256), 2304 + p32,
        np.arange(2816, 6912),
        np.arange(512, 768), np.arange(1920, 2048), np.arange(2560, 2816)])
    assert cols.shape[0] == INW2
    L['w_in'] = np.ascontiguousarray(w_in[:, :, cols])
    L['w_mod'] = inp['w_mod']
    L['b_mod'] = inp['b_mod']
    L['g12'] = np.ascontiguousarray(np.stack([inp['norm1_g'], inp['norm2_g']], axis=1))
    L['final_g'] = inp['final_g'].reshape(1, D)
    rpb = inp['na_rpb']
    ck = np.arange(64)[:, None]
    cq = np.arange(64)[None, :]
    cs = np.clip(cq - 8, 0, 48)
    valid = (ck >= cs) & (ck < cs + 16)
    dc = np.clip(ck - cq + 15, 0, 30)
    tc = np.where(valid[None, None, None], rpb[:, :, :, dc], np.float32(NEG))
    top = tc[:, :, 0:14].transpose(0, 1, 3, 2, 4)
    bot = tc[:, :, 1:15].transpose(0, 1, 3, 2, 4)
    L['na_tc'] = np.ascontiguousarray(np.concatenate([top, bot], axis=2).transpose(0, 2, 1, 3, 4)).astype(np.float32)
    sw = inp['hy_short_w']
    sb = inp['hy_short_b']
    swb = np.concatenate([sw, sb[:, None, :]], axis=1)
    L['hy_sw'] = np.ascontiguousarray(swb.reshape(DEPTH, 4, 6, 128).transpose(0, 3, 2, 1))
    L['hy_w1'] = inp['hy_w1']
    L['hy_w2'] = inp['hy_w2']
    L['hy_w3'] = inp['hy_w3']
    L['hy_w4'] = inp['hy_w4']
    hb = np.stack([inp['hy_b1'], inp['hy_b2'], inp['hy_b3']], axis=1)
    L['hy_bf'] = np.ascontiguousarray(np.concatenate([hb, inp['hy_freq']], axis=1).transpose(0, 2, 1))
    L['hy_bias'] = np.ascontiguousarray(inp['hy_bias'].reshape(DEPTH, 2, 128).transpose(0, 2, 1))
    pq = _perm(64, 32)
    L['gqa_g'] = np.ascontiguousarray(np.stack([inp['gqa_qn'], inp['gqa_qn'][:, pq], inp['gqa_kn'], inp['gqa_kn'][:, pq]], axis=2))
    L['diff_l'] = np.ascontiguousarray(np.stack([inp['diff_lq1'], inp['diff_lk1'], inp['diff_lq2'], inp['diff_lk2']], axis=1))
    L['diff_subln'] = inp['diff_subln']
    L['w_branch'] = inp['w_branch']
    L['w_out'] = inp['w_out']
    L['peer_wq'] = inp['peer_wq']
    keys = inp['peer_keys']
    kb = np.zeros((DEPTH, 128, 8, 256), np.float32)
    for p in range(2):
        kb[:, p * 64:(p + 1) * 64, :, p * 128:(p + 1) * 128] = keys[:, :, p].transpose(0, 3, 1, 2)
    L['peer_kb'] = kb
    L['peer_u'] = inp['peer_u']
    L['peer_v'] = inp['peer_v']
    return L


class _Off:
    def __init__(self, t, off):
        self.t, self.off = t, off

    def __getitem__(self, key):
        key = list(key)
        key[1] = key[1] - self.off
        return self.t[tuple(key)]


def dram_bc(ap1d, p):
    n = ap1d.size()
    return bass.AP(tensor=ap1d.tensor, offset=ap1d.offset, ap=[[0, p], [1, n]])


def sb_view(tile_ap, col_off, dims):
    base = tile_ap
    return bass.AP(tensor=base.tensor, offset=base.offset + col_off, ap=[list(base.ap[0])] + [list(d) for d in dims])


def row_type(i):
    b, j = divmod(i, 18)
    return 2 if j < 2 else b


class Prog:
    def __init__(self, n_layers=DEPTH, debug=False, stages=None):
        self.n_layers = n_layers
        self.debug = debug
        self.stages = stages
        self.nc = bass.Bass("TRN2", target_bir_lowering=False)
        self.es = ExitStack()
        self.kb = KB(self.nc, self.es)
        self.inputs = {}

    def ext(self, name, arr_shape, dt=F32):
        ap = self.kb.dram(name, list(arr_shape), dt, 'ExternalInput')
        self.inputs[name] = (tuple(arr_shape), dt)
        return ap

    def scratch(self, name, shape, dt):
        return self.kb.dram(name, list(shape), dt, 'ExternalOutput' if self.debug else 'Internal')

    def want(self, s):
        return self.stages is None or s in self.stages

    def build(self, shapes):
        kb = self.kb
        es = self.es
        I = {}
        for k, (shp, dt) in shapes.items():
            I[k] = self.ext(k, shp, dt)
        self.I = I
        self.out = kb.dram('out', [NB, LAT, D], F32, 'ExternalOutput')
        self.h = self.scratch('h', [T, D], F32)
        self.modx = self.scratch('modx', [3, 6144], F32)
        self.PT = self.scratch('PT', [NFM, T], F32)
        self.PV = self.scratch('PV', [T, NTM], F32)
        self.YT = self.scratch('YT', [1024, T], BF16)
        self.Hl = self.scratch('Hl', [128, NF // 128, 2, 256], F32)
        self.Hc = self.scratch('Hc', [128, NFC // 128, 2, 256], F32)
        self.UT = kb.dram('UT', [128, 8, 16384], BF16)
        self.VB = kb.dram('VB', [16384, 1024], BF16)
        self.PS = [kb.psum(es, 'ps%d' % i, [128, 512], F32) for i in range(8)]
        self.ident_f = kb.sbuf(es, 'identf', [128, 128], F32)
        self.ident_b = kb.sbuf(es, 'identb', [128, 128], BF16)
        self.ones_f = kb.sbuf(es, 'onesf', [128, 128], F32)
        kb.dma(self.ident_f[:], I['ident'])
        kb.copy(self.ident_b[:], self.ident_f[:])
        kb.memset(self.ones_f[:], 1.0)
        for i in range(4):
            r0 = i * (T // 4)
            kb.dma(self.h[r0:r0 + T // 4, :], I['h0'][r0:r0 + T // 4, :], q=('sp', 'pool')[i % 2])
        kb.barrier()
        for l in range(self.n_layers):
            need_ctx = l < DEPTH - 1
            if self.want('mod'):
                self.st_mod(l)
            if self.want('inproj'):
                self.st_inproj(l)
            if self.want('filt'):
                self.st_filters(l, need_ctx)
            if self.want('attn'):
                self.st_attn(l, need_ctx)
            if self.want('hyena'):
                self.st_hyena(l, need_ctx)
            if self.want('merge'):
                self.st_merge(l)
            if self.want('peer'):
                self.st_peer(l, need_ctx)
        if self.want('final'):
            self.st_final()
        kb.finish()
        return self.nc

    def psb(self, i):
        return self.PS[i][:].bitcast(BF16)

    def st_mod(self, l):
        kb, I = self.kb, self.I
        with ExitStack() as st:
            ct = kb.sbuf(st, 'ct', [128, 8, 3], F32)
            kb.dma(ct[:], I['cT'])
            sc = kb.sbuf(st, 'sc', [128, 8, 3], F32)
            kb.act(sc[:], ct[:], AF.Silu)
            mv = kb.sbuf(st, 'mv', [3, 6144], F32)
            bm = kb.sbuf(st, 'bm', [3, 6144], F32)
            kb.dma(bm[:], dram_bc(I['b_mod'][l], 3))
            gt = kb.sbuf(st, 'gt', [3, 2048], F32)
            kb.dma(gt[:], dram_bc(I['g12'][l].rearrange('a d -> (a d)'), 3))
            wts = [kb.sbuf(st, 'wm%d' % i, [128, 8, 512], F32) for i in range(2)]
            wv = I['w_mod'][l].rearrange('(kc p) n -> p kc n', p=128)
            for n in range(12):
                wt = wts[n % 2]
                kb.dma(wt[:], wv[:, :, n * 512:(n + 1) * 512], q=('sp', 'pool')[n % 2])
                ps = self.PS[n % 2]
                for kc in range(8):
                    kb.mm(ps[0:3, :], sc[:, kc, :], wt[:, kc, :], start=(kc == 0), stop=(kc == 7))
                kb.tt(mv[:, n * 512:(n + 1) * 512], ps[0:3, :], bm[:, n * 512:(n + 1) * 512], ALU.add)
            kb.stt(mv[:, 1024:2048], mv[:, 1024:2048], 1.0, gt[:, 0:1024], ALU.add, ALU.mult)
            kb.stt(mv[:, 4096:5120], mv[:, 4096:5120], 1.0, gt[:, 1024:2048], ALU.add, ALU.mult)
            kb.dma(self.modx, mv[:])
            kb.barrier()

    def load_bc(self, st, name, col0, n=1024):
        kb = self.kb
        t = kb.sbuf(st, name, [128, 3, n], F32)
        for r in range(3):
            kb.dma(t[:, r, :], dram_bc(self.modx[r, col0:col0 + n], 128), q=('sp', 'pool', 'sp')[r])
        return t

    def norm_tile(self, x, x2, junk, ss, src, A, B, out_bf):
        kb = self.kb
        kb.dma(x[:], src)
        kb.act(junk[:], x[:], AF.Square, accum_out=ss[:])
        kb.ts(ss[:], ss[:], 1.0 / D, EPS, op0=ALU.mult, op1=ALU.add)
        kb.act(ss[:], ss[:], AF.Sqrt)
        kb.recip(ss[:], ss[:])
        kb.stt(x2[:], x[:], ss[:], A, ALU.mult, ALU.mult)
        kb.tt(out_bf, x2[:], B, ALU.add)

    def st_inproj(self, l):
        kb, I = self.kb, self.I
        with ExitStack() as st:
            nT = kb.sbuf(st, 'nT', [128, 8, T], BF16)
            with ExitStack() as s2:
                A = self.load_bc(s2, 'A1', 1024)
                B = self.load_bc(s2, 'B1', 0)
                xs = [kb.sbuf(s2, 'x%d' % i, [128, D], F32) for i in range(2)]
                x2s = [kb.sbuf(s2, 'x2%d' % i, [128, D], F32) for i in range(2)]
                nbs = [kb.sbuf(s2, 'nb%d' % i, [128, D], BF16) for i in range(2)]
                junk = kb.sbuf(s2, 'junk', [128, D], BF16)
                sss = [kb.sbuf(s2, 'ss%d' % i, [128, 1], F32) for i in range(2)]
                for i in range(NT):
                    r = row_type(i)
                    nb = nbs[i % 2]
                    self.norm_tile(xs[i % 2], x2s[i % 2], junk, sss[i % 2], self.h[i * 128:(i + 1) * 128, :],
                                   A[:, r, :], B[:, r, :], nb[:])
                    pT = self.psb(i % 2)
                    for kc in range(8):
                        kb.tr(pT[:, kc * 128:(kc + 1) * 128], nb[:, kc * 128:(kc + 1) * 128], self.ident_b[:])
                    kb.copy(nT[:, :, i * 128:(i + 1) * 128], pT.rearrange('p (k t) -> p k t', k=8),
                            eng=('act', 'dve')[i % 2])
                kb.barrier()
            wv = I['w_in'][l].rearrange('(kc p) n -> p kc n', p=128)
            w32 = [kb.sbuf(st, 'w32%d' % i, [128, 8, 512], F32) for i in range(2)]
            wb = [kb.sbuf(st, 'wb%d' % i, [128, 8, 512], BF16) for i in range(2)]
            o32 = [kb.sbuf(st, 'o32%d' % i, [128, 512], F32) for i in range(4)]
            cnt = 0
            for cg in range(NFM // 512):
                w3 = w32[cg % 2]
                w2 = wb[cg % 2]
                kb.dma(w3[:, 0:4, :], wv[:, 0:4, cg * 512:(cg + 1) * 512], q='sp')
                kb.dma(w3[:, 4:8, :], wv[:, 4:8, cg * 512:(cg + 1) * 512], q='pool')
                kb.copy(w2[:, 0:4, :], w3[:, 0:4, :], eng='dve')
                kb.copy(w2[:, 4:8, :], w3[:, 4:8, :], eng='act')
                for blk in range(T // 512):
                    for fc in range(4):
                        ps = self.PS[2 + cnt % 4]
                        for kc in range(8):
                            kb.mm(ps[:, :], w2[:, kc, fc * 128:(fc + 1) * 128], nT[:, kc, blk * 512:(blk + 1) * 512],
                                  start=(kc == 0), stop=(kc == 7))
                        o = o32[cnt % 4]
                        kb.copy(o[:], ps[:, :], eng=('act', 'dve')[cnt % 2])
                        row = (cg * 4 + fc) * 128
                        kb.dma(self.PT[row:row + 128, blk * 512:(blk + 1) * 512], o[:], q=('sp', 'pool')[cnt % 2])
                        cnt += 1
            kb.dma(w32[0][:], wv[:, :, NFM:NFM + 512], q='sp')
            kb.dma(w32[1][:, :, 0:128], wv[:, :, NFM + 512:NFM + 640], q='pool')
            kb.copy(wb[0][:], w32[0][:], eng='dve')
            kb.copy(wb[1][:, :, 0:128], w32[1][:, :, 0:128], eng='act')
            ov = [kb.sbuf(st, 'ov%d' % i, [128, NTM], F32) for i in range(2)]
            for i in range(NT):
                pa = self.PS[2 + (2 * i) % 4]
                pb = self.PS[3 + (2 * i) % 4]
                for kc in range(8):
                    kb.mm(pa[:, :], nT[:, kc, i * 128:(i + 1) * 128], wb[0][:, kc, :], start=(kc == 0), stop=(kc == 7))
                for kc in range(8):
                    kb.mm(pb[:, 0:128], nT[:, kc, i * 128:(i + 1) * 128], wb[1][:, kc, 0:128], start=(kc == 0), stop=(kc == 7))
                o = ov[i % 2]
                kb.copy(o[:, 0:512], pa[:, :], eng='act')
                kb.copy(o[:, 512:640], pb[:, 0:128], eng='dve')
                kb.dma(self.PV[i * 128:(i + 1) * 128, :], o[:], q=('sp', 'pool')[i % 2])
            kb.barrier()

    def attn_block(self, QT, q0, nq, chunks, po, ebufs, pbase):
        kb = self.kb
        n = len(chunks)
        for i, (kt, vx, bias) in enumerate(chunks):
            ps = self.PS[pbase + self._sc % 3]
            e = ebufs[self._sc % 3]
            self._sc += 1
            kb.mm(ps[:, 0:nq], kt, QT[:, q0:q0 + nq], start=True, stop=(bias is None))
            if bias is not None:
                kb.mm(ps[:, 0:nq], self.ident_b[:], bias, start=False, stop=True)
            kb.act(e[:, 0:nq], ps[:, 0:nq], AF.Exp)
            self._q.append((lambda vx=vx, e=e, i=i: kb.mm(po, vx, e[:, 0:nq], start=(i == 0), stop=(i == n - 1))))
            self.attn_pump(2)

    def attn_pump(self, keep):
        while len(self._q) > keep:
            self._q.pop(0)()

    def attn_fin(self, po, nq, oT, dests, rc):
        def fin():
            self._attn_fin_now(po, nq, self._oTs[self._fn % 2], dests, self._rcs[self._fn % 2])
            self._fn += 1
        self._q.append(fin)

    def attn_flush(self):
        self.attn_pump(0)

    def _attn_fin_now(self, po, nq, oT, dests, rc):
        kb = self.kb
        kb.copy(oT[0:65, 0:nq], po, eng='act')
        for sbi, dst in enumerate(dests):
            pt = self.PS[5 + self._fc % 2]
            self._fc += 1
            kb.tr(pt[:, 0:65], oT[0:65, sbi * 128:(sbi + 1) * 128], self.ident_f[0:65, 0:65])
            kb.recip(rc[:], pt[:, 64:65])
            kb.ts(dst, pt[:, 0:64], rc[:], None, op0=ALU.mult)

    def tm_to_fm(self, Y, ntile, row0, col0, stg):
        kb = self.kb
        for cc in range(2):
            for g0 in range(0, ntile, 4):
                g = min(4, ntile - g0)
                pt = self.PS[5 + self._fc % 2]
                self._fc += 1
                for j in range(g):
                    kb.tr(pt[:, j * 128:(j + 1) * 128], Y[:, g0 + j, cc * 128:(cc + 1) * 128], self.ident_f[:])
                kb.copy(stg[:, cc, g0 * 128:(g0 + g) * 128], pt[:, 0:g * 128], eng=('act', 'dve')[(g0 // 4) % 2])
            kb.dma(self.YT[row0 + cc * 128:row0 + (cc + 1) * 128, col0:col0 + ntile * 128], stg[:, cc, 0:ntile * 128],
                   q=('sp', 'pool')[cc])

    def prep_rope(self, dst, xrows, xrrows, Ct, St, rms, tmp):
        kb = self.kb
        x, xr = tmp['x'], tmp['xr']
        kb.dma(x[:], xrows, q='sp')
        kb.dma(xr[:], xrrows, q='pool')
        if rms:
            sq, rs = tmp['sq'], tmp['rs']
            kb.act(sq[:], x[:], AF.Square)
            for c0 in range(0, TPB, 512):
                n = min(512, TPB - c0)
                ps = self.PS[7]
                kb.mm(ps[0:64, 0:n], self.ones_f[0:64, 0:64], sq[:, c0:c0 + n])
                kb.ts(rs[:, c0:c0 + n], ps[0:64, 0:n], 1.0 / 64, EPS, op0=ALU.mult, op1=ALU.add)
            kb.act(rs[:], rs[:], AF.Sqrt)
            kb.recip(rs[:], rs[:])
        kb.tt(x[:], x[:], Ct, ALU.mult)
        kb.tt(xr[:], xr[:], St, ALU.mult, eng='pool')
        if rms:
            kb.tt(x[:], x[:], xr[:], ALU.add)
            kb.tt(dst, x[:], rs[:], ALU.mult)
        else:
            kb.tt(dst, x[:], xr[:], ALU.add)

    def load_vx(self, VX, vst, row0, nchunk, col0, nh):
        kb = self.kb
        kb.dma(vst[:, 0:nchunk, 0:nh * 64],
               self.PV[row0:row0 + nchunk * 128, col0:col0 + nh * 64].rearrange('(kc p) c -> p kc c', p=128))
        for hh in range(nh):
            kb.copy(VX[:, 0:nchunk, hh, 0:64], vst[:, 0:nchunk, hh * 64:(hh + 1) * 64], eng=('dve', 'pool')[hh % 2])
        kb.memset(VX[:, :, :, 64:65], 1.0)

    def st_attn(self, l, need_ctx):
        kb, I = self.kb, self.I
        self._sc = 0
        self._fc = 0
        lam_init = 0.8 - 0.6 * math.exp(-0.3 * l)
        with ExitStack() as st:
            ebufs = [kb.sbuf(st, 'e%d' % i, [128, 512], BF16) for i in range(3)]
            oT = kb.sbuf(st, 'oT', [65, 512], F32)
            rc = kb.sbuf(st, 'rc', [128, 1], F32)
            self._oTs = [oT, kb.sbuf(st, 'oT2', [65, 512], F32)]
            self._rcs = [rc, kb.sbuf(st, 'rc2', [128, 1], F32)]
            self._q = []
            self._fn = 0
            Y = kb.sbuf(st, 'Y', [128, 18, 256], F32)
            stg = kb.sbuf(st, 'stg', [128, 2, TPB], BF16)
            vst = kb.sbuf(st, 'vst', [128, 18, 256], F32)
            VX = kb.sbuf(st, 'VX', [128, 18, 4, 65], BF16)
            x32 = kb.sbuf(st, 'x32', [64, TPB], F32)
            tmp = dict(x=x32, xr=kb.sbuf(st, 'xr32', [64, TPB], F32), sq=kb.sbuf(st, 'sq32', [64, TPB], F32),
                       rs=kb.sbuf(st, 'rs32', [64, TPB], F32))
            KT = kb.sbuf(st, 'KT', [64, 4, TPB], BF16)
            QT = kb.sbuf(st, 'QT', [64, TPB], BF16)
            with ExitStack() as s2:
                VXb = kb.sbuf(s2, 'VXb', [128, 15, 4, 65], BF16)
                tc32 = kb.sbuf(s2, 'tc32', [128, 4, 14, 64], F32)
                TC = kb.sbuf(s2, 'TC', [128, 4, 14, 64], BF16)
                kb.dma(tc32[:], I['na_tc'][l])
                kb.copy(TC[:], tc32[:])
                for b in range(NB):
                    c0 = b * TPB
                    for hh in range(4):
                        kb.dma(x32[:], self.PT[R_NAK + hh * 64:R_NAK + (hh + 1) * 64, c0:c0 + TPB], q=('sp', 'pool')[hh % 2])
                        kb.copy(KT[:, hh, :], x32[:], eng=('dve', 'act')[hh % 2])
                    self.load_vx(VX, vst, c0, 18, 0, 4)
                    self.load_vx(VXb, vst, c0 + CTX + 64, 15, 0, 4)
                    for hh in range(4):
                        kb.dma(x32[:], self.PT[R_NAQ + hh * 64:R_NAQ + (hh + 1) * 64, c0:c0 + TPB], q=('sp', 'pool')[hh % 2])
                        kb.ts(QT[:], x32[:], 0.125, None, op0=ALU.mult)
                        for m in range(16):
                            po = self.PS[3 + m % 2]
                            for r in (2 * m, 2 * m + 1):
                                rs_ = min(max(r - 4, 0), 24)
                                chunks = [(KT[:, hh, kc * 128:(kc + 1) * 128], VX[:, kc, hh, :], None) for kc in range(2)]
                                for i in range(4):
                                    kr = rs_ + 2 * i
                                    kt = KT[:, hh, CTX + kr * 64:CTX + kr * 64 + 128]
                                    vx = VX[:, 2 + kr // 2, hh, :] if kr % 2 == 0 else VXb[:, (kr - 1) // 2, hh, :]
                                    chunks.append((kt, vx, TC[:, hh, kr - r + 7, :]))
                                self.attn_block(QT, CTX + r * 64, 64, chunks, po[0:65, (r % 2) * 64:(r % 2) * 64 + 64], ebufs, 0)
                            self.attn_fin(po[0:65, 0:128], 128, oT, [Y[:, 2 + m, hh * 64:(hh + 1) * 64]], rc)
                        if need_ctx:
                            po = self.PS[3]
                            chunks = [(KT[:, hh, kc * 128:(kc + 1) * 128], VX[:, kc, hh, :], None) for kc in range(2)]
                            self.attn_block(QT, 0, 256, chunks, po[0:65, 0:256], ebufs, 0)
                            self.attn_fin(po[0:65, 0:256], 256, oT, [Y[:, j, hh * 64:(hh + 1) * 64] for j in range(2)], rc)
                    self.attn_flush()
                    if need_ctx:
                        self.tm_to_fm(Y, 18, 0, c0, stg)
                    else:
                        self.tm_to_fm(sb_view(Y[:], 2 * 256, [[256, 16], [1, 256]]), 16, 0, c0 + CTX, stg)
                kb.barrier()
            with ExitStack() as s2:
                gg = kb.sbuf(s2, 'gg', [64, 4], F32)
                kb.dma(gg[:], I['gqa_g'][l])
                tabs = kb.sbuf(s2, 'gtabs', [64, 4, TPB], F32)
                for j in range(4):
                    kb.dma(tabs[:, j, :], I['rope'][j % 2], q=('sp', 'pool')[j % 2])
                kb.ts(tabs[:, 0, :], tabs[:, 0, :], gg[:, 0:1], 0.125, op0=ALU.mult, op1=ALU.mult)
                kb.ts(tabs[:, 1, :], tabs[:, 1, :], gg[:, 1:2], 0.125, op0=ALU.mult, op1=ALU.mult)
                kb.ts(tabs[:, 2, :], tabs[:, 2, :], gg[:, 2:3], None, op0=ALU.mult)
                kb.ts(tabs[:, 3, :], tabs[:, 3, :], gg[:, 3:4], None, op0=ALU.mult)
                for b in range(NB):
                    c0 = b * TPB
                    for kv in range(2):
                        self.prep_rope(KT[:, kv, :], self.PT[R_GK + kv * 64:R_GK + (kv + 1) * 64, c0:c0 + TPB],
                                       self.PT[R_GKR + kv * 64:R_GKR + (kv + 1) * 64, c0:c0 + TPB],
                                       tabs[:, 2, :], tabs[:, 3, :], True, tmp)
                    self.load_vx(VX, vst, c0, 18, 256, 2)
                    for hh in range(4):
                        kv = hh // 2
                        self.prep_rope(QT[:], self.PT[R_GQ + hh * 64:R_GQ + (hh + 1) * 64, c0:c0 + TPB],
                                       self.PT[R_GQR + hh * 64:R_GQR + (hh + 1) * 64, c0:c0 + TPB],
                                       tabs[:, 0, :], tabs[:, 1, :], True, tmp)
                        blocks = [(CTX + i * 512, 512, 18) for i in range(4)]
                        if need_ctx:
                            blocks.append((0, 256, 2))
                        for bi, (q0, nq, nk) in enumerate(blocks):
                            po = self.PS[3 + bi % 2]
                            chunks = [(KT[:, kv, kc * 128:(kc + 1) * 128], VX[:, kc, kv, :], None) for kc in range(nk)]
                            self.attn_block(QT, q0, nq, chunks, po[0:65, 0:nq], ebufs, 0)
                            self.attn_fin(po[0:65, 0:nq], nq, oT,
                                          [Y[:, q0 // 128 + j, hh * 64:(hh + 1) * 64] for j in range(nq // 128)], rc)
                    self.attn_flush()
                    if need_ctx:
                        self.tm_to_fm(Y, 18, 512, c0, stg)
                    else:
                        self.tm_to_fm(sb_view(Y[:], 2 * 256, [[256, 16], [1, 256]]), 16, 512, c0 + CTX, stg)
                kb.barrier()
            with ExitStack() as s2:
                rope = kb.sbuf(s2, 'rope', [64, 6, TPB], F32) if False else _Off(kb.sbuf(s2, 'rope', [64, 4, TPB], F32), 2)
                for j in range(2, 6):
                    kb.dma(rope[:, j, :], I['rope'][j], q=('sp', 'pool')[j % 2])
                dl = kb.sbuf(s2, 'dl', [128, 4, 32], F32)
                kb.dma(dl[:], dram_bc(I['diff_l'][l].rearrange('a d -> (a d)'), 128))
                lw = kb.sbuf(s2, 'lw', [128, 2, 32], F32)
                kb.tt(lw[:, 0, :], dl[:, 0, :], dl[:, 1, :], ALU.mult)
                kb.tt(lw[:, 1, :], dl[:, 2, :], dl[:, 3, :], ALU.mult)
                l2 = kb.sbuf(s2, 'l2', [128, 2], F32)
                kb.reduce(l2[:], lw[:], ALU.add, AX.X)
                kb.act(l2[:], l2[:], AF.Exp)
                nlam = kb.sbuf(s2, 'nlam', [128, 1], F32)
                kb.tt(nlam[:], l2[:, 1:2], l2[:, 0:1], ALU.subtract)
                kb.ts(nlam[:], nlam[:], -lam_init, None, op0=ALU.add)
                gsub = kb.sbuf(s2, 'gsub', [128, 64], F32)
                kb.dma(gsub[:], dram_bc(I['diff_subln'][l], 128))
                kb.ts(gsub[:], gsub[:], 1.0 - lam_init, None, op0=ALU.mult)
                a12 = kb.sbuf(s2, 'a12', [128, 2, 4, 64], F32)
                osq = kb.sbuf(s2, 'osq', [128, 64], F32)
                oss = kb.sbuf(s2, 'oss', [128, 1], F32)
                for b in range(NB):
                    c0 = b * TPB
                    for hh in range(4):
                        self.prep_rope(KT[:, hh, :], self.PT[R_DK + hh * 64:R_DK + (hh + 1) * 64, c0:c0 + TPB],
                                       self.PT[R_DKR + hh * 64:R_DKR + (hh + 1) * 64, c0:c0 + TPB],
                                       rope[:, 4, :], rope[:, 5, :], False, tmp)
                    self.load_vx(VX, vst, c0, 18, 384, 4)
                    for hh in range(4):
                        self.prep_rope(QT[:], self.PT[R_DQ + hh * 64:R_DQ + (hh + 1) * 64, c0:c0 + TPB],
                                       self.PT[R_DQR + hh * 64:R_DQR + (hh + 1) * 64, c0:c0 + TPB],
                                       rope[:, 2, :], rope[:, 3, :], False, tmp)
                        blocks = [(CTX + i * 512, 512, 18) for i in range(4)]
                        if need_ctx:
                            blocks.append((0, 256, 2))
                        for bi, (q0, nq, nk) in enumerate(blocks):
                            nsb = nq // 128
                            for pr in range(2):
                                po = self.PS[3 + pr]
                                chunks = [(KT[pr * 32:(pr + 1) * 32, hh, kc * 128:(kc + 1) * 128], VX[:, kc, hh, :], None)
                                          for kc in range(nk)]
                                self.attn_block(QT[pr * 32:(pr + 1) * 32, :], q0, nq, chunks, po[0:65, 0:nq], ebufs, 0)
                                self.attn_fin(po[0:65, 0:nq], nq, oT, [a12[:, pr, j, :] for j in range(nsb)], rc)
                            self.attn_flush()
                            for j in range(nsb):
                                o = a12[:, 0, j, :]
                                kb.stt(o, a12[:, 1, j, :], nlam[:], o, ALU.mult, ALU.add)
                                kb.act(osq[:], o, AF.Square, accum_out=oss[:])
                                kb.ts(oss[:], oss[:], 1.0 / 64, EPS, op0=ALU.mult, op1=ALU.add)
                                kb.act(oss[:], oss[:], AF.Sqrt)
                                kb.recip(oss[:], oss[:])
                                kb.stt(Y[:, q0 // 128 + j, hh * 64:(hh + 1) * 64], o, oss[:], gsub[:], ALU.mult, ALU.mult)
                    if need_ctx:
                        self.tm_to_fm(Y, 18, 768, c0, stg)
                    else:
                        self.tm_to_fm(sb_view(Y[:], 2 * 256, [[256, 16], [1, 256]]), 16, 768, c0 + CTX, stg)
                kb.barrier()

    def fm_to_tm_bf(self, src_fn, nsc, dst_fn):
        kb = self.kb
        for g0 in range(0, nsc, 8):
            g = min(8, nsc - g0)
            pT = self.psb(5 + self._fc % 2)
            self._fc += 1
            for j in range(g):
                kb.tr(pT[:, j * 128:(j + 1) * 128], src_fn(g0 + j), self.ident_b[:])
            kb.copy(dst_fn(g0, g), pT[:, 0:g * 128].rearrange('p (k t) -> p k t', k=g), eng=('act', 'dve')[(g0 // 8) % 2])

    def st_filters(self, l, need_ctx):
        kb, I = self.kb, self.I
        self._fc = 0
        seqs = [('lat', LAT, NF, self.Hl)]
        if need_ctx:
            seqs.append(('ctx', CTX, NFC, self.Hc))
        for nm, n, nf, Hd in seqs:
            nsc = n // 128
            nfc = nf // 128
            with ExitStack() as st:
                zT = kb.sbuf(st, 'hz', [33, n], F32)
                kb.dma(zT[:], I['hz_' + nm])
                w1 = kb.sbuf(st, 'hw1', [33, 64], F32)
                w2 = kb.sbuf(st, 'hw2', [64, 64], F32)
                w3 = kb.sbuf(st, 'hw3', [64, 64], F32)
                w4 = kb.sbuf(st, 'hw4', [64, 512], F32)
                bf = kb.sbuf(st, 'hbf', [64, 6], F32)
                kb.dma(w1[:], I['hy_w1'][l])
                kb.dma(w2[:], I['hy_w2'][l])
                kb.dma(w3[:], I['hy_w3'][l])
                kb.dma(w4[:], I['hy_w4'][l])
                kb.dma(bf[:], I['hy_bf'][l])
                ha = kb.sbuf(st, 'ha', [64, n], F32)
                hb = kb.sbuf(st, 'hb', [64, n], F32)
                tm = kb.sbuf(st, 'htm', [64, n], F32)
                hb2 = kb.sbuf(st, 'hb2', [64, n], F32)
                hb3 = kb.sbuf(st, 'hb3', [64, n], F32)
                bs = min(512, n)

                def layer(inp, K, w, outp, j):
                    for c0 in range(0, n, bs):
                        ps = self.PS[(c0 // bs) % 2]
                        kb.mm(ps[0:64, 0:bs], w[0:K, :], inp[0:K, c0:c0 + bs])
                        kb.ts(tm[:, c0:c0 + bs], ps[0:64, 0:bs], bf[:, j:j + 1], bf[:, 3 + j:4 + j], op0=ALU.add, op1=ALU.mult)
                    for _ in range(2):
                        kb.ts(hb2[:], tm[:], math.pi, -2.0 * math.pi, op0=ALU.is_gt, op1=ALU.mult)
                        kb.ts(hb3[:], tm[:], -math.pi, 2.0 * math.pi, op0=ALU.is_lt, op1=ALU.mult)
                        kb.tt(tm[:], tm[:], hb2[:], ALU.add)
                        kb.tt(tm[:], tm[:], hb3[:], ALU.add)
                    kb.act(outp[:], tm[:], AF.Sin)

                layer(zT, 33, w1, ha, 0)
                layer(ha, 64, w2, hb, 1)
                layer(hb, 64, w3, ha, 2)
                hh = kb.sbuf(st, 'hh', [128, 4, n], F32)
                win = kb.sbuf(st, 'hwin', [128, 4, n], F32)
                kb.dma(win[:], I['hwin_' + nm])
                for cc in range(4):
                    for c0 in range(0, n, bs):
                        ps = self.PS[2 + (c0 // bs) % 2]
                        kb.mm(ps[:, 0:bs], w4[:, cc * 128:(cc + 1) * 128], ha[:, c0:c0 + bs])
                        kb.tt(hh[:, cc, c0:c0 + bs], ps[:, 0:bs], win[:, cc, c0:c0 + bs], ALU.mult)
                nrm = kb.sbuf(st, 'hnrm', [128, 4], F32)
                kb.reduce(nrm[:], hh[:], ALU.add, AX.X, absval=True)
                kb.recip(nrm[:], nrm[:])
                for cc in range(4):
                    kb.ts(hh[:, cc, :], hh[:, cc, :], nrm[:, cc:cc + 1], None, op0=ALU.mult)
                hsd = kb.sbuf(st, 'hsd', [128, 2, 2, n], BF16)
                kb.tt(hsd[:, 0, :, :], hh[:, 0:2, :], hh[:, 2:4, :], ALU.add)
                kb.tt(hsd[:, 1, :, :], hh[:, 2:4, :], hh[:, 0:2, :], ALU.subtract)
                hT = kb.sbuf(st, 'hT', [128, nsc, 2, 256], BF16)
                for kind in range(2):
                    for cc in range(2):
                        self.fm_to_tm_bf(lambda sc: hsd[:, kind, cc, sc * 128:(sc + 1) * 128], nsc,
                                         lambda g0, g: hT[:, g0:g0 + g, kind, cc * 128:(cc + 1) * 128])
                wg = kb.sbuf(st, 'hwg', [128, nfc], F32)
                kb.dma(wg[:], I['wgt_' + nm])
                Hs = kb.sbuf(st, 'Hs', [128, nfc, 2, 256], F32)
                cfs = [kb.sbuf(st, 'cf%d' % i, [128, nsc, 128], BF16) for i in range(2)]
                sfs = [kb.sbuf(st, 'sf%d' % i, [128, nsc, 128], BF16) for i in range(2)]
                for fc in range(nfc):
                    cf, sf = cfs[fc % 2], sfs[fc % 2]
                    kb.dma(cf[:], I['cfwd_' + nm][fc], q='sp')
                    kb.dma(sf[:], I['sfwd_' + nm][fc], q='pool')
                    pr = self.PS[(2 * fc) % 4]
                    pi = self.PS[(2 * fc + 1) % 4]
                    for sc in range(nsc):
                        kb.mm(pr[:, 0:256], cf[:, sc, :], hT[:, sc, 0, :], start=(sc == 0), stop=(sc == nsc - 1))
                    for sc in range(nsc):
                        kb.mm(pi[:, 0:256], sf[:, sc, :], hT[:, sc, 1, :], start=(sc == 0), stop=(sc == nsc - 1))
                    kb.ts(Hs[:, fc, 0, :], pr[:, 0:256], wg[:, fc:fc + 1], None, op0=ALU.mult)
                    kb.ts(Hs[:, fc, 1, :], pi[:, 0:256], wg[:, fc:fc + 1], None, op0=ALU.mult)
                kb.dma(Hd, Hs[:])
                kb.barrier()

    def st_hyena(self, l, need_ctx):
        kb, I = self.kb, self.I
        self._fc = 0
        seqs = [('lat', LAT, NF, self.Hl, CTX)]
        if need_ctx:
            seqs.append(('ctx', CTX, NFC, self.Hc, 0))
        for nm, n, nf, Hd, toff in seqs:
            nsc = n // 128
            nfc = nf // 128
            bs = min(512, n)
            for b in range(NB):
                c0 = b * TPB + toff
                with ExitStack() as st:
                    sw = kb.sbuf(st, 'sw', [128, 6, 4], F32)
                    kb.dma(sw[:], I['hy_sw'][l])
                    hbias = kb.sbuf(st, 'hbias', [128, 2], F32)
                    kb.dma(hbias[:], I['hy_bias'][l])
                    uc = kb.sbuf(st, 'uc', [128, 6, n], F32)
                    zb = kb.sbuf(st, 'zb', [128, 2, n], BF16)
                    zT = kb.sbuf(st, 'zT', [128, nsc, 256], BF16)
                    PQ = kb.sbuf(st, 'PQ', [128, nfc, 2, 256], BF16)
                    with ExitStack() as s2:
                        us = [kb.sbuf(s2, 'u%d' % i, [128, n], F32) for i in range(2)]
                        for cc in range(6):
                            u = us[cc % 2]
                            kb.dma(u[:], self.PT[R_HY + cc * 128:R_HY + (cc + 1) * 128, c0:c0 + n], q=('sp', 'pool')[cc % 2])
                            kb.ts(uc[:, cc, :], u[:], sw[:, cc, 1:2], sw[:, cc, 3:4], op0=ALU.mult, op1=ALU.add)
                            kb.stt(uc[:, cc, 1:n], u[:, 0:n - 1], sw[:, cc, 0:1], uc[:, cc, 1:n], ALU.mult, ALU.add)
                            kb.stt(uc[:, cc, 0:n - 1], u[:, 1:n], sw[:, cc, 2:3], uc[:, cc, 0:n - 1], ALU.mult, ALU.add)
                        kb.tt(uc[:, 2:4, :], uc[:, 2:4, :], uc[:, 4:6, :], ALU.mult)
                        kb.copy(zb[:], uc[:, 2:4, :], eng='act')
                        for cc in range(2):
                            self.fm_to_tm_bf(lambda sc: zb[:, cc, sc * 128:(sc + 1) * 128], nsc,
                                             lambda g0, g: zT[:, g0:g0 + g, cc * 128:(cc + 1) * 128])
                        cfs = [kb.sbuf(s2, 'cf%d' % i, [128, nsc, 128], BF16) for i in range(2)]
                        sfs = [kb.sbuf(s2, 'sf%d' % i, [128, nsc, 128], BF16) for i in range(2)]
                        Hts = [kb.sbuf(s2, 'Ht%d' % i, [128, 2, 256], F32) for i in range(2)]
                        t1 = kb.sbuf(s2, 't1', [128, 256], F32)
                        t2 = kb.sbuf(s2, 't2', [128, 256], F32)
                        for fc in range(nfc):
                            cf, sf, Ht = cfs[fc % 2], sfs[fc % 2], Hts[fc % 2]
                            kb.dma(cf[:], I['cfwd_' + nm][fc], q='sp')
                            kb.dma(sf[:], I['sfwd_' + nm][fc], q='pool')
                            kb.dma(Ht[:], Hd[:, fc, :, :], q='sp')
                            pr = self.PS[(2 * fc) % 4]
                            pi = self.PS[(2 * fc + 1) % 4]
                            for sc in range(nsc):
                                kb.mm(pr[:, 0:256], cf[:, sc, :], zT[:, sc, :], start=(sc == 0), stop=(sc == nsc - 1))
                            for sc in range(nsc):
                                kb.mm(pi[:, 0:256], sf[:, sc, :], zT[:, sc, :], start=(sc == 0), stop=(sc == nsc - 1))
                            kb.tt(t1[:], pr[:, 0:256], Ht[:, 0, :], ALU.mult)
                            kb.tt(t2[:], pi[:, 0:256], Ht[:, 1, :], ALU.mult)
                            kb.tt(PQ[:, fc, 0, :], t1[:], t2[:], ALU.add)
                            kb.tt(t1[:], pi[:, 0:256], Ht[:, 0, :], ALU.mult)
                            kb.tt(t2[:], pr[:, 0:256], Ht[:, 1, :], ALU.mult)
                            kb.tt(PQ[:, fc, 1, :], t1[:], t2[:], ALU.subtract)
                        kb.barrier()
                    with ExitStack() as s2:
                        crs = kb.sbuf(s2, 'cr', [128, nfc, bs], BF16)
                        srs = kb.sbuf(s2, 'sr', [128, nfc, bs], BF16)
                        yb = kb.sbuf(s2, 'yb', [128, 2, n], BF16)
                        t3 = kb.sbuf(s2, 't3', [128, bs], F32)
                        for t0 in range(0, n, bs):
                            kb.dma(crs[:], I['crow_' + nm][:, t0:t0 + bs].rearrange('(fc p) t -> p fc t', p=128), q='sp')
                            kb.dma(srs[:], I['srow_' + nm][:, t0:t0 + bs].rearrange('(fc p) t -> p fc t', p=128), q='pool')
                            for cc in range(2):
                                ps = self.PS[cc]
                                for fc in range(nfc):
                                    kb.mm(ps[:, 0:bs], PQ[:, fc, 0, cc * 128:(cc + 1) * 128], crs[:, fc, :], start=(fc == 0), stop=False)
                                    kb.mm(ps[:, 0:bs], PQ[:, fc, 1, cc * 128:(cc + 1) * 128], srs[:, fc, :], start=False, stop=(fc == nfc - 1))
                                kb.stt(t3[:, 0:bs], uc[:, 2 + cc, t0:t0 + bs], hbias[:, cc:cc + 1], ps[:, 0:bs], ALU.mult, ALU.add)
                                kb.tt(yb[:, cc, t0:t0 + bs], t3[:, 0:bs], uc[:, cc, t0:t0 + bs], ALU.mult)
                        for cc in range(2):
                            kb.dma(self.YT[256 + cc * 128:256 + (cc + 1) * 128, c0:c0 + n], yb[:, cc, :], q=('sp', 'pool')[cc])
                        kb.barrier()

    def st_merge(self, l):
        kb, I = self.kb, self.I
        with ExitStack() as st:
            wbr = kb.sbuf(st, 'wbr', [128, 8, D], BF16)
            wo = kb.sbuf(st, 'wo', [128, 8, D], BF16)
            wst = [kb.sbuf(st, 'wst%d' % i, [128, D], F32) for i in range(2)]
            wbv = I['w_branch'][l].rearrange('k (wc p) d -> p (k wc) d', p=128)
            wov = I['w_out'][l].rearrange('(kc p) d -> p kc d', p=128)
            for j in range(8):
                kb.dma(wst[0][:], wbv[:, j, :], q='sp')
                kb.copy(wbr[:, j, :], wst[0][:], eng='dve')
                kb.dma(wst[1][:], wov[:, j, :], q='pool')
                kb.copy(wo[:, j, :], wst[1][:], eng='act')
            G1 = self.load_bc(st, 'G1', 2048)
            yT = kb.sbuf(st, 'yT', [128, 8, 512], BF16)
            mixT = kb.sbuf(st, 'mixT', [128, 8, 512], BF16)
            gls = [kb.sbuf(st, 'gl%d' % i, [128, 512], F32) for i in range(3)]
            sgs = [kb.sbuf(st, 'sg%d' % i, [128, 512], F32) for i in range(2)]
            acc = kb.sbuf(st, 'macc', [128, 512], F32)
            tmps = [kb.sbuf(st, 'mtmp%d' % i, [128, 512], F32) for i in range(2)]
            hts = [kb.sbuf(st, 'ht%d' % i, [128, D], F32) for i in range(2)]
            cnt = 0
            tgen = self.peer_tables_gen(l, st) if self.want('peer') else iter(())
            for blk in range(T // 512):
                t0 = blk * 512
                kb.dma(yT[:], self.YT[:, t0:t0 + 512].rearrange('(j p) t -> p j t', p=128))
                for dc in range(8):
                    for _ in range(2):
                        next(tgen, None)
                    for k in range(4):
                        ps = self.PS[cnt % 3]
                        gl = gls[cnt % 3]
                        sg = sgs[cnt % 2]
                        tmp = tmps[cnt % 2]
                        cnt += 1
                        for wc in range(2):
                            kb.mm(ps[:, :], wbr[:, k * 2 + wc, dc * 128:(dc + 1) * 128], yT[:, k * 2 + wc, :],
                                  start=(wc == 0), stop=(wc == 1))
                        row = R_GATE + k * 1024 + dc * 128
                        kb.dma(gl[:], self.PT[row:row + 128, t0:t0 + 512], q=('sp', 'pool')[cnt % 2])
                        kb.act(sg[:], gl[:], AF.Sigmoid)
                        if k == 0:
                            kb.tt(acc[:], ps[:, :], sg[:], ALU.mult)
                        elif k < 3:
                            kb.tt(tmp[:], ps[:, :], sg[:], ALU.mult)
                            kb.tt(acc[:], acc[:], tmp[:], ALU.add, eng='pool')
                        else:
                            kb.tt(tmp[:], ps[:, :], sg[:], ALU.mult)
                            kb.tt(mixT[:, dc, :], acc[:], tmp[:], ALU.add, eng='pool')
                for ti in range(4):
                    i = blk * 4 + ti
                    r = row_type(i)
                    ht = hts[i % 2]
                    kb.dma(ht[:], self.h[i * 128:(i + 1) * 128, :], q='sp')
                    for half in range(2):
                        ps = self.PS[4 + half + 2 * (i % 2)]
                        for dc in range(8):
                            kb.mm(ps[:, :], mixT[:, dc, ti * 128:(ti + 1) * 128], wo[:, dc, half * 512:(half + 1) * 512],
                                  start=(dc == 0), stop=(dc == 7))
                        tmp = tmps[half]
                        kb.tt(tmp[:], ps[:, :], G1[:, r, half * 512:(half + 1) * 512], ALU.mult)
                        kb.tt(ht[:, half * 512:(half + 1) * 512], ht[:, half * 512:(half + 1) * 512], tmp[:], ALU.add)
                    kb.dma(self.h[i * 128:(i + 1) * 128, :], ht[:], q='pool')
            for _ in tgen:
                pass
            kb.barrier()

    def peer_tables_gen(self, l, st):
        kb, I = self.kb, self.I
        u32 = [kb.sbuf(st, 'u32%d' % i, [128, D], F32) for i in range(2)]
        ub = [kb.sbuf(st, 'ub%d' % i, [128, D], BF16) for i in range(2)]
        uT = [kb.sbuf(st, 'uT%d' % i, [128, 8, 128], BF16) for i in range(2)]
        v32 = [kb.sbuf(st, 'v32%d' % i, [128, D], F32) for i in range(2)]
        vb = [kb.sbuf(st, 'vb%d' % i, [128, D], BF16) for i in range(2)]
        for ec in range(128):
            k = ec % 2
            kb.dma(u32[k][:], I['peer_u'][l, ec * 128:(ec + 1) * 128, :], q='sp')
            kb.copy(ub[k][:], u32[k][:], eng='dve')
            pT = self.psb(3)
            for kc in range(8):
                kb.tr(pT[:, kc * 128:(kc + 1) * 128], ub[k][:, kc * 128:(kc + 1) * 128], self.ident_b[:])
            kb.copy(uT[k][:], pT.rearrange('p (k e) -> p k e', k=8), eng='act')
            kb.dma(self.UT[:, :, ec * 128:(ec + 1) * 128], uT[k][:], q='sp')
            kb.dma(v32[k][:], I['peer_v'][l, ec * 128:(ec + 1) * 128, :], q='pool')
            kb.copy(vb[k][:], v32[k][:], eng='pool')
            kb.dma(self.VB[ec * 128:(ec + 1) * 128, :], vb[k][:], q='pool')
            yield

    def st_peer(self, l, need_ctx):
        kb, I = self.kb, self.I
        tiles = [i for i in range(NT) if need_ctx or (i % 18) >= 2]
        GT = 4
        groups = [tiles[i:i + GT] for i in range(0, len(tiles), GT)]
        with ExitStack() as st:
            wq = kb.sbuf(st, 'wq', [128, 8, D], BF16)
            kbk = kb.sbuf(st, 'kbk', [128, 8, 256], BF16)
            with ExitStack() as s0:
                wst = [kb.sbuf(s0, 'pwst%d' % i, [128, D], F32) for i in range(2)]
                wqv = I['peer_wq'][l].rearrange('(kc p) d -> p kc d', p=128)
                for j in range(8):
                    kb.dma(wst[j % 2][:], wqv[:, j, :], q=('sp', 'pool')[j % 2])
                    kb.copy(wq[:, j, :], wst[j % 2][:], eng=('dve', 'act')[j % 2])
                k32 = kb.sbuf(s0, 'k32', [128, 8, 256], F32)
                kb.dma(k32[:], I['peer_kb'][l])
                kb.copy(kbk[:], k32[:])
                kb.barrier()
            mT = kb.sbuf(st, 'mT', [128, 8, GT * 128], BF16)
            S = [kb.sbuf(st, 'S%d' % i, [128, 2048], F32) for i in range(GT)]
            negb = kb.sbuf(st, 'negb', [128, GT, 8], F32)
            thr = kb.sbuf(st, 'thr', [128, GT, 8], F32)
            oacc = [kb.sbuf(st, 'oacc%d' % i, [128, D], F32) for i in range(GT)]
            Abc = kb.sbuf(st, 'Abc', [128, D], F32)
            Bbc = kb.sbuf(st, 'Bbc', [128, D], F32)
            x = kb.sbuf(st, 'px', [128, D], F32)
            x2 = kb.sbuf(st, 'px2', [128, D], F32)
            for grp in groups:
                nt = len(grp)
                ntok = nt * 128
                with ExitStack() as s1:
                    junk = kb.sbuf(s1, 'pjunk', [128, D], BF16)
                    qT = kb.sbuf(s1, 'qT', [128, 8, GT * 128], BF16)
                    mb = kb.sbuf(s1, 'pmb', [128, D], BF16)
                    ss = kb.sbuf(s1, 'pss', [128, 1], F32)
                    sv = kb.sbuf(s1, 'sv', [128, 16, 16], F32)
                    stmp = kb.sbuf(s1, 'stmp', [128, 2048], F32)
                    cand = kb.sbuf(s1, 'cand', [128, 8, 256], F32)
                    ctmp = kb.sbuf(s1, 'ctmp', [128, 8, 256], F32)
                    best = kb.sbuf(s1, 'best', [128, 8, 16], F32)
                    eb = kb.sbuf(s1, 'eb', [128, 8, 16], F32)
                    zs = kb.sbuf(s1, 'zs', [128, 8], F32)
                    for gi, i in enumerate(grp):
                        r = row_type(i)
                        kb.dma(Abc[:], dram_bc(self.modx[r, 4096:5120], 128), q='pool')
                        kb.dma(Bbc[:], dram_bc(self.modx[r, 3072:4096], 128), q='pool')
                        self.norm_tile(x, x2, junk, ss, self.h[i * 128:(i + 1) * 128, :], Abc[:], Bbc[:], mb[:])
                        pT = self.psb(gi % 2)
                        for kc in range(8):
                            kb.tr(pT[:, kc * 128:(kc + 1) * 128], mb[:, kc * 128:(kc + 1) * 128], self.ident_b[:])
                        kb.copy(mT[:, :, gi * 128:(gi + 1) * 128], pT.rearrange('p (k t) -> p k t', k=8), eng='act')
                    for hc in range(8):
                        ps = self.PS[2 + hc % 2]
                        for kc in range(8):
                            kb.mm(ps[:, 0:ntok], wq[:, kc, hc * 128:(hc + 1) * 128], mT[:, kc, 0:ntok],
                                  start=(kc == 0), stop=(kc == 7))
                        kb.copy(qT[:, hc, 0:ntok], ps[:, 0:ntok], eng=('act', 'dve')[hc % 2])
                    for gi in range(nt):
                        Sg = S[gi]
                        for hp in range(4):
                            ps = self.PS[4 + hp % 2]
                            for j in range(2):
                                hh = 2 * hp + j
                                kb.mm(ps[:, j * 256:(j + 1) * 256], qT[:, hh, gi * 128:(gi + 1) * 128], kbk[:, hh, :])
                            kb.copy(Sg[:, hp * 512:(hp + 1) * 512], ps[:, :], eng='act')
                        for g in range(16):
                            sl = slice(g * 128, (g + 1) * 128)
                            kb.max8(sv[:, g, 0:8], Sg[:, sl])
                            kb.match_replace(stmp[:, sl], sv[:, g, 0:8], Sg[:, sl], -1e30)
                            kb.max8(sv[:, g, 8:16], stmp[:, sl])
                        kb.tt(sb_view(cand[:], 0, [[256, 8], [16, 16], [1, 16]]),
                              sb_view(sv[:], 0, [[32, 8], [1, 16], [0, 16]]),
                              sb_view(sv[:], 16, [[32, 8], [0, 16], [1, 16]]), ALU.add)
                        for hh in range(8):
                            kb.max8(best[:, hh, 0:8], cand[:, hh, :])
                            kb.match_replace(ctmp[:, hh, :], best[:, hh, 0:8], cand[:, hh, :], -1e30)
                            kb.max8(best[:, hh, 8:16], ctmp[:, hh, :])
                        kb.tt(eb[:], best[:], sb_view(best[:], 0, [[16, 8], [0, 16]]), ALU.subtract)
                        kb.act(eb[:], eb[:], AF.Exp)
                        kb.reduce(zs[:], eb[:], ALU.add, AX.X)
                        kb.act(zs[:], zs[:], AF.Ln)
                        kb.tt(negb[:, gi, :], zs[:], sb_view(best[:], 0, [[16, 8]]), ALU.add)
                        kb.ts(negb[:, gi, :], negb[:, gi, :], -1.0, None, op0=ALU.mult)
                        kb.copy(thr[:, gi, :], sb_view(best[:], 15, [[16, 8]]))
                    kb.barrier()
                with ExitStack() as s2:
                    ut = [kb.sbuf(s2, 'ut%d' % i, [128, 8, 1024], BF16) for i in range(2)]
                    vt = [kb.sbuf(s2, 'vt%d' % i, [128, 8, 1024], BF16) for i in range(2)]
                    NBF = 4
                    sm = [kb.sbuf(s2, 'sm%d' % i, [128, 1024], F32) for i in range(NBF)]
                    ee = [kb.sbuf(s2, 'ee%d' % i, [128, 1024], BF16) for i in range(NBF)]
                    gm = [kb.sbuf(s2, 'gm%d' % i, [128, 1024], BF16) for i in range(NBF)]
                    ga = [kb.sbuf(s2, 'ga%d' % i, [128, 1024], BF16) for i in range(2)]
                    wt = [kb.sbuf(s2, 'wt%d' % i, [128, 1024], BF16) for i in range(2)]
                    wT = [kb.sbuf(s2, 'wT%d' % i, [128, 8, 128], BF16) for i in range(2)]
                    otmp = [kb.sbuf(s2, 'otmp%d' % i, [128, 512], F32) for i in range(2)]
                    cnt = 0
                    it = 0
                    pend = None
                    for ec in range(16):
                        u_t, v_t = ut[ec % 2], vt[ec % 2]
                        kb.dma(u_t[:], self.UT[:, :, ec * 1024:(ec + 1) * 1024], q='sp')
                        kb.dma(v_t[:], self.VB[ec * 1024:(ec + 1) * 1024, :].rearrange('(j p) d -> p j d', p=128), q='pool')
                        for gi in range(nt):
                            Sg = S[gi]
                            k2 = it % 2
                            it += 1
                            for half in range(2):
                                for kc in range(8):
                                    kb.mm(self.PS[half][:, :], mT[:, kc, gi * 128:(gi + 1) * 128],
                                          u_t[:, kc, half * 512:(half + 1) * 512], start=(kc == 0), stop=(kc == 7))
                            sums = {}

                            def emit_sum(hh, Sg=Sg, ec=ec):
                                nonlocal cnt
                                k = cnt % NBF
                                cnt += 1
                                kb.tt(sb_view(sm[k][:], 0, [[128, 8], [1, 128]]),
                                      sb_view(Sg[:], hh * 256 + ec * 8, [[1, 8], [0, 128]]),
                                      sb_view(Sg[:], hh * 256 + 128, [[0, 8], [1, 128]]), ALU.add, eng='dve')
                                sums[hh] = k

                            emit_sum(0)
                            emit_sum(1)
                            for hh in range(8):
                                k = sums[hh]
                                kb.act(ee[k][:], sm[k][:], AF.Exp, bias=negb[:, gi, hh:hh + 1])
                                if hh + 2 < 8:
                                    emit_sum(hh + 2)
                                kb.stt(gm[k][:], sm[k][:], thr[:, gi, hh:hh + 1], ee[k][:], ALU.is_ge, ALU.mult)
                                for half in range(2):
                                    kb.mm(self.PS[2 + half][:, :], self.ident_b[:], gm[k][:, half * 512:(half + 1) * 512],
                                          start=(hh == 0), stop=(hh == 7))
                                if hh == 3:
                                    for half in range(2):
                                        kb.act(ga[k2][:, half * 512:(half + 1) * 512], self.PS[half][:, :], AF.Gelu)
                            for half in range(2):
                                kb.tt(wt[k2][:, half * 512:(half + 1) * 512], self.PS[2 + half][:, :],
                                      ga[k2][:, half * 512:(half + 1) * 512], ALU.mult)
                            if pend is not None:
                                pend()

                            def tail(k2=k2, v_t=v_t, gi=gi, ec=ec):
                                pT = self.psb(6)
                                for j in range(8):
                                    kb.tr(pT[:, j * 128:(j + 1) * 128], wt[k2][:, j * 128:(j + 1) * 128], self.ident_b[:])
                                kb.copy(wT[k2][:], pT.rearrange('p (j t) -> p j t', j=8), eng='act')
                                for half in range(2):
                                    po = self.PS[4 + half]
                                    for j in range(8):
                                        kb.mm(po[:, :], wT[k2][:, j, :], v_t[:, j, half * 512:(half + 1) * 512],
                                              start=(j == 0), stop=(j == 7))
                                    oa = oacc[gi][:, half * 512:(half + 1) * 512]
                                    if ec == 0:
                                        kb.copy(oa, po[:, :], eng='act')
                                    else:
                                        ot = otmp[half]
                                        kb.copy(ot[:], po[:, :], eng='act')
                                        kb.tt(oa, oa, ot[:], ALU.add, eng='pool')
                            pend = tail
                    if pend is not None:
                        pend()
                        pend = None
                    kb.barrier()
                for gi, i in enumerate(grp):
                    r = row_type(i)
                    kb.dma(Abc[:], dram_bc(self.modx[r, 5120:6144], 128), q='pool')
                    kb.dma(x[:], self.h[i * 128:(i + 1) * 128, :], q='sp')
                    kb.tt(x2[:], oacc[gi][:], Abc[:], ALU.mult)
                    kb.tt(x[:], x[:], x2[:], ALU.add)
                    kb.dma(self.h[i * 128:(i + 1) * 128, :], x[:], q='sp')
            kb.barrier()

    def st_final(self):
        kb, I = self.kb, self.I
        with ExitStack() as st:
            g = kb.sbuf(st, 'fg', [128, D], F32)
            kb.dma(g[:], dram_bc(I['final_g'].rearrange('a d -> (a d)'), 128))
            xs = [kb.sbuf(st, 'fx%d' % i, [128, D], F32) for i in range(2)]
            ys = [kb.sbuf(st, 'fy%d' % i, [128, D], F32) for i in range(2)]
            junk = kb.sbuf(st, 'fjunk', [128, D], BF16)
            sss = [kb.sbuf(st, 'fss%d' % i, [128, 1], F32) for i in range(2)]
            k = 0
            for b in range(NB):
                for j in range(LAT // 128):
                    i = b * 18 + 2 + j
                    x, y, ss = xs[k % 2], ys[k % 2], sss[k % 2]
                    kb.dma(x[:], self.h[i * 128:(i + 1) * 128, :], q='sp')
                    kb.act(junk[:], x[:], AF.Square, accum_out=ss[:])
                    kb.ts(ss[:], ss[:], 1.0 / D, EPS, op0=ALU.mult, op1=ALU.add)
                    kb.act(ss[:], ss[:], AF.Sqrt)
                    kb.recip(ss[:], ss[:])
                    kb.stt(y[:], x[:], ss[:], g[:], ALU.mult, ALU.mult)
                    kb.dma(self.out[b, j * 128:(j + 1) * 128, :], y[:], q='pool')
                    k += 1
            kb.barrier()


def core_inputs(inp, L, C, core):
    b0 = core * NB
    d = {}
    d['h0'] = np.ascontiguousarray(np.concatenate(
        [np.concatenate([inp['ctx'][b], inp['x'][b]], axis=0) for b in range(b0, b0 + NB)], axis=0))
    crow = np.stack([inp['c'][b0], inp['c'][b0 + 1], inp['c_ctx']])
    d['cT'] = np.ascontiguousarray(crow.reshape(3, 8, 128).transpose(2, 1, 0))
    d.update(L)
    d.update(C)
    return d


def in_shapes(d):
    sh = {}
    for k, v in d.items():
        sh[k] = (v.shape, BF16 if v.dtype == ml_dtypes.bfloat16 else F32)
    return sh


_CACHE = {}


def kernel(**inputs):
    inp = {k: np.asarray(v) for k, v in inputs.items()}
    L = host_layout(inp)
    C = _CACHE.get('consts')
    if C is None:
        C = host_consts()
        _CACHE['consts'] = C
    cis = [core_inputs(inp, L, C, c) for c in range(8)]
    prog = Prog()
    nc = prog.build(in_shapes(cis[0]))
    res = run_bass_kernel_spmd(nc, cis, core_ids=list(range(8)))
    out = np.concatenate([np.asarray(r['out']) for r in res.results], axis=0)
    return np.ascontiguousarray(out.astype(np.float32))
```
